# Optimizing a Trainium2 kernel written in Bass

```python
import jax, jax.numpy as jnp
from jax import lax
import numpy as np

D_MODEL = 1024
BATCH = 8
SEQ = 4096
DEPTH = 4

HEAD_DIM = 128
EPS = 1e-6
ROPE_THETA = 500000.0
ROT_DIM = HEAD_DIM // 4
N_MEM = 256
MEM_HEADS = 4
DIL_GROUPS = ((128, 1), (512, 4), (2048, 16))
A_HEADS = 4
POOL_SIZES = (2, 4, 8, 16)
POOL_CH = 128
CHUNK = 128
C_GROUPS = 8
C_CH = 128

A_WIDTH = A_HEADS * HEAD_DIM
B_WIDTH = len(POOL_SIZES) * POOL_CH
C_WIDTH = C_GROUPS * C_CH
M_WIDTH = MEM_HEADS * HEAD_DIM
EVEN_MIX = A_WIDTH + B_WIDTH + M_WIDTH
ODD_MIX = C_WIDTH + M_WIDTH
A_QK_WIDTH = 2 * len(DIL_GROUPS) * A_WIDTH
EVEN_IN = A_QK_WIDTH + A_WIDTH + B_WIDTH + M_WIDTH + EVEN_MIX
ODD_IN = 2 * C_WIDTH + M_WIDTH + ODD_MIX
N_EVEN = (DEPTH + 1) // 2
N_ODD = DEPTH // 2

kernel_name = "hybrid_dilated_pool_gmlp_trunk"


def rmsnorm(x, g):
    xf = x.astype(jnp.float32)
    y = xf * lax.rsqrt(jnp.mean(xf * xf, axis=-1, keepdims=True) + EPS)
    return (y * g.astype(jnp.float32)).astype(x.dtype)


def layernorm(x, g, b):
    xf = x.astype(jnp.float32)
    mu = jnp.mean(xf, axis=-1, keepdims=True)
    xc = xf - mu
    y = xc * lax.rsqrt(jnp.mean(xc * xc, axis=-1, keepdims=True) + EPS)
    return (y * g.astype(jnp.float32) + b.astype(jnp.float32)).astype(x.dtype)


def rope_tables(positions):
    pos = positions.astype(jnp.float32)
    inv = ROPE_THETA ** (-jnp.arange(0, ROT_DIM, 2, dtype=jnp.float32) / ROT_DIM)
    ang = pos[..., None] * inv
    return jnp.cos(ang), jnp.sin(ang)


def apply_partial_rope(x, cos, sin):
    half = ROT_DIM // 2
    xr = x[..., :ROT_DIM].astype(jnp.float32)
    x1, x2 = xr[..., :half], xr[..., half:]
    c, s = cos[:, :, None, :], sin[:, :, None, :]
    rot = jnp.concatenate([x1 * c - x2 * s, x2 * c + x1 * s], axis=-1).astype(x.dtype)
    return jnp.concatenate([rot, x[..., ROT_DIM:]], axis=-1)


def _to_strided(x, dil, blk):
    b, s, h, d = x.shape
    unit = dil * blk
    lp = -(-s // unit) * unit
    x = jnp.pad(x, ((0, 0), (0, lp - s), (0, 0), (0, 0)))
    x = x.reshape(b, lp // dil, dil, h, d).transpose(0, 2, 1, 3, 4)
    return x.reshape(b, dil, lp // unit, blk, h, d)


def _from_strided(x, seq):
    b, dil, nb, blk = x.shape[:4]
    rest = x.shape[4:]
    x = jnp.moveaxis(x.reshape((b, dil, nb * blk) + rest), 1, 2)
    return x.reshape((b, nb * blk * dil) + rest)[:, :seq]


def dilated_window_group(q, k, v, window, dil):
    span = window // dil
    blk = span
    seq = q.shape[1]
    qb, kb, vb = (_to_strided(t, dil, blk) for t in (q, k, v))
    nb = qb.shape[2]

    def with_prev(t):
        prev = jnp.concatenate([jnp.zeros_like(t[:, :, :1]), t[:, :, :-1]], axis=2)
        return jnp.concatenate([prev, t], axis=3)

    kk, vv = with_prev(kb), with_prev(vb)
    scores = jnp.einsum('brnqhd,brnkhd->brnhqk', qb, kk,
                        preferred_element_type=jnp.float32) * (HEAD_DIM ** -0.5)
    qi = jnp.arange(blk)[:, None]
    kj = jnp.arange(2 * blk)[None, :]
    off = blk + qi - kj
    band = (off >= 0) & (off <= span)
    not_before_start = (jnp.arange(nb)[:, None, None] > 0) | (kj[None] >= blk)
    valid = band[None] & not_before_start
    scores = jnp.where(valid[None, None, :, None], scores, -jnp.inf)
    m = jnp.max(scores, axis=-1)
    p = jnp.exp(scores - m[..., None])
    den = jnp.sum(p, axis=-1)
    num = jnp.einsum('brnhqk,brnkhd->brnqhd', p, vv.astype(jnp.float32))
    m = jnp.swapaxes(m, 3, 4)
    den = jnp.swapaxes(den, 3, 4)
    return _from_strided(num, seq), _from_strided(m, seq), _from_strided(den, seq)


def dilated_attention_mixer(qk, v_a, cos, sin):
    nums, ms, dens = [], [], []
    for gi, (window, dil) in enumerate(DIL_GROUPS):
        q = apply_partial_rope(qk[:, :, gi, 0], cos, sin)
        k = apply_partial_rope(qk[:, :, gi, 1], cos, sin)
        num, m, den = dilated_window_group(q, k, v_a, window, dil)
        nums.append(num); ms.append(m); dens.append(den)
    ms = jnp.stack(ms)
    wts = jnp.exp(ms - jnp.max(ms, axis=0, keepdims=True))
    num = sum(wts[g][..., None] * nums[g] for g in range(len(DIL_GROUPS)))
    den = jnp.sum(wts * jnp.stack(dens), axis=0)
    out = num / den[..., None]
    b, s = out.shape[:2]
    return out.reshape(b, s, A_WIDTH).astype(v_a.dtype)


def multiscale_pool(xp, w_pool, scale):
    b, s, _ = xp.shape
    xg = xp.reshape(b, s, len(POOL_SIZES), POOL_CH).astype(jnp.float32)
    c = jnp.cumsum(xg, axis=1)
    t = jnp.arange(s)
    outs = []
    for gi, w in enumerate(POOL_SIZES):
        cg = c[:, :, gi]
        lag = jnp.pad(cg, ((0, 0), (w, 0), (0, 0)))[:, :s]
        cnt = jnp.minimum(t + 1, w).astype(jnp.float32)[None, :, None]
        outs.append((cg - lag) / cnt - xg[:, :, gi])
    pooled = jnp.stack(outs, axis=2)
    y = jnp.einsum('bsgc,gcd->bsgd', pooled, w_pool.astype(jnp.float32))
    return (y.reshape(b, s, B_WIDTH) * scale.astype(jnp.float32)).astype(xp.dtype)


def chunked_spatial_gating(u, v, ln_g, ln_b, w_s, b_s):
    b, s, _ = u.shape
    vn = layernorm(v, ln_g, ln_b)
    vc = vn.reshape(b, s // CHUNK, CHUNK, C_GROUPS, C_CH)
    tril = jnp.tril(jnp.ones((CHUNK, CHUNK), dtype=bool))
    ws = jnp.where(tril[None], w_s, jnp.zeros_like(w_s))
    mixed = jnp.einsum('gts,bnsgc->bntgc', ws, vc) + b_s.T[None, None, :, :, None]
    return u * mixed.reshape(b, s, C_WIDTH)


def memory_attention(q_m, mem_n, w_mem_kv):
    b, s = q_m.shape[:2]
    q = q_m.reshape(b, s, MEM_HEADS, HEAD_DIM)
    mk, mv = jnp.split(mem_n @ w_mem_kv, 2, axis=-1)
    mk = mk.reshape(b, -1, MEM_HEADS, HEAD_DIM)
    mv = mv.reshape(b, -1, MEM_HEADS, HEAD_DIM)
    sc = jnp.einsum('bshd,bmhd->bhsm', q, mk,
                    preferred_element_type=jnp.float32) * (HEAD_DIM ** -0.5)
    p = jax.nn.softmax(sc, axis=-1)
    out = jnp.einsum('bhsm,bmhd->bshd', p, mv.astype(jnp.float32))
    return out.reshape(b, s, M_WIDTH).astype(q_m.dtype)


def even_layer(x, cos, sin, mem_n, g_norm, w_in, w_pool, pool_scale, w_mem_kv, w_out):
    b, s, _ = x.shape
    h = rmsnorm(x, g_norm)
    proj = h @ w_in
    cuts = np.cumsum([A_QK_WIDTH, A_WIDTH, B_WIDTH, M_WIDTH]).tolist()
    qk, v_a, x_b, q_m, z = jnp.split(proj, cuts, axis=-1)
    qk = qk.reshape(b, s, len(DIL_GROUPS), 2, A_HEADS, HEAD_DIM)
    v_a = v_a.reshape(b, s, A_HEADS, HEAD_DIM)
    a_out = dilated_attention_mixer(qk, v_a, cos, sin)
    b_out = multiscale_pool(x_b, w_pool, pool_scale)
    m_out = memory_attention(q_m, mem_n, w_mem_kv)
    y = jnp.concatenate([a_out, b_out, m_out], axis=-1) * jax.nn.silu(z)
    return x + y @ w_out


def odd_layer(x, mem_n, g_norm, w_in, ln_g, ln_b, w_s, b_s, w_mem_kv, w_out):
    h = rmsnorm(x, g_norm)
    proj = h @ w_in
    cuts = np.cumsum([C_WIDTH, C_WIDTH, M_WIDTH]).tolist()
    u, v, q_m, z = jnp.split(proj, cuts, axis=-1)
    c_out = chunked_spatial_gating(u, v, ln_g, ln_b, w_s, b_s)
    m_out = memory_attention(q_m, mem_n, w_mem_kv)
    y = jnp.concatenate([c_out, m_out], axis=-1) * jax.nn.silu(z)
    return x + y @ w_out


def setup_inputs(seed: int = 0) -> dict:
    key = jax.random.key(seed)
    ks = jax.random.split(key, 24)
    nrm = jax.random.normal
    f32 = jnp.float32
    offs = jax.random.randint(ks[2], (BATCH, 1), 0, 1024)
    positions = (jnp.arange(SEQ, dtype=jnp.int32)[None, :] + offs).astype(jnp.int32)
    return {
        "x": nrm(ks[0], (BATCH, SEQ, D_MODEL), f32),
        "mem": nrm(ks[1], (BATCH, N_MEM, D_MODEL), f32),
        "positions": positions,
        "g_mem": 1.0 + 0.02 * nrm(ks[3], (D_MODEL,), f32),
        "even_norm_g": 1.0 + 0.02 * nrm(ks[4], (N_EVEN, D_MODEL), f32),
        "even_w_in": nrm(ks[5], (N_EVEN, D_MODEL, EVEN_IN), f32) * D_MODEL ** -0.5,
        "even_w_pool": nrm(ks[6], (N_EVEN, len(POOL_SIZES), POOL_CH, POOL_CH), f32) * POOL_CH ** -0.5,
        "even_pool_scale": 1.0 + 0.02 * nrm(ks[7], (N_EVEN, B_WIDTH), f32),
        "even_w_mem_kv": nrm(ks[8], (N_EVEN, D_MODEL, 2 * M_WIDTH), f32) * D_MODEL ** -0.5,
        "even_w_out": nrm(ks[9], (N_EVEN, EVEN_MIX, D_MODEL), f32) * EVEN_MIX ** -0.5,
        "odd_norm_g": 1.0 + 0.02 * nrm(ks[10], (N_ODD, D_MODEL), f32),
        "odd_w_in": nrm(ks[11], (N_ODD, D_MODEL, ODD_IN), f32) * D_MODEL ** -0.5,
        "odd_ln_g": 1.0 + 0.02 * nrm(ks[12], (N_ODD, C_WIDTH), f32),
        "odd_ln_b": 0.02 * nrm(ks[13], (N_ODD, C_WIDTH), f32),
        "odd_w_s": nrm(ks[14], (N_ODD, C_GROUPS, CHUNK, CHUNK), f32) * CHUNK ** -0.5,
        "odd_b_s": 1.0 + 0.02 * nrm(ks[15], (N_ODD, C_GROUPS, CHUNK), f32),
        "odd_w_mem_kv": nrm(ks[16], (N_ODD, D_MODEL, 2 * M_WIDTH), f32) * D_MODEL ** -0.5,
        "odd_w_out": nrm(ks[17], (N_ODD, ODD_MIX, D_MODEL), f32) * ODD_MIX ** -0.5,
        "final_norm_g": 1.0 + 0.02 * nrm(ks[18], (D_MODEL,), f32),
    }


def reference(x, mem, positions, g_mem, even_norm_g, even_w_in, even_w_pool, even_pool_scale,
              even_w_mem_kv, even_w_out, odd_norm_g, odd_w_in, odd_ln_g, odd_ln_b, odd_w_s,
              odd_b_s, odd_w_mem_kv, odd_w_out, final_norm_g):
    cos, sin = rope_tables(positions)
    mem_n = rmsnorm(mem, g_mem)
    for layer in range(DEPTH):
        i = layer // 2
        if layer % 2 == 0:
            x = even_layer(x, cos, sin, mem_n, even_norm_g[i], even_w_in[i], even_w_pool[i],
                           even_pool_scale[i], even_w_mem_kv[i], even_w_out[i])
        else:
            x = odd_layer(x, mem_n, odd_norm_g[i], odd_w_in[i], odd_ln_g[i], odd_ln_b[i],
                          odd_w_s[i], odd_b_s[i], odd_w_mem_kv[i], odd_w_out[i])
    return rmsnorm(x, final_norm_g)
```

```python
import contextlib
import numpy as np
import ml_dtypes
import concourse.bass as bass
import concourse.mybir as mybir
from concourse.bass_utils import run_bass_kernel_spmd

F32 = mybir.dt.float32
BF16 = mybir.dt.bfloat16
I32 = mybir.dt.int32
AF = mybir.ActivationFunctionType
ALU = mybir.AluOpType

S = 4096
D = 1024
NST = 8
NTB = 32
EPS = 1e-6
DILS = (1, 4, 16)
N_DSEM = 24
SCALE = 128 ** -0.5
NEG = -30000.0


class Op:
    __slots__ = ("eng", "fn", "deps", "is_dma", "idx", "signal", "sigcount", "dsem", "dval")


class Prog:
    ENGS = ("pe", "act", "dve", "pool", "sp")

    def __init__(self):
        self.ops = {e: [] for e in self.ENGS}
        self.last_w = {}
        self.readers = {}
        self.dsem_last = [None] * N_DSEM
        self.dsem_cnt = [0] * N_DSEM
        self.n_dma = 0
        self.bank_ctr = 0

    def bank(self):
        b = self.bank_ctr % 8
        self.bank_ctr += 1
        return b

    def op(self, eng, fn, reads=(), writes=(), dma=False):
        o = Op()
        o.eng = eng; o.fn = fn; o.is_dma = dma
        o.idx = len(self.ops[eng]); o.signal = False; o.sigcount = 0
        o.dsem = None; o.dval = 0
        deps = []
        for k in reads:
            excl = isinstance(k, str) and k.startswith("ps")
            w = self.last_w.get(k)
            if w is not None:
                deps.append(w)
            if excl:
                deps.extend(r for r in self.readers.get(k, ()) if r.eng != eng)
        for k in writes:
            w = self.last_w.get(k)
            if w is not None:
                deps.append(w)
            deps.extend(self.readers.get(k, ()))
        if dma:
            s = self.n_dma % N_DSEM
            self.n_dma += 1
            prev = self.dsem_last[s]
            if prev is not None:
                deps.append(prev)
            self.dsem_cnt[s] += 16
            o.dsem = s; o.dval = self.dsem_cnt[s]
            self.dsem_last[s] = o
        fd = []
        seen = set()
        for d in deps:
            if id(d) in seen:
                continue
            seen.add(id(d))
            if (not d.is_dma) and d.eng == eng:
                if eng == "pe" or eng == "sp":
                    continue
                if o.idx - d.idx > 3:
                    continue
            fd.append(d)
            if not d.is_dma:
                d.signal = True
        o.deps = fd
        for k in reads:
            self.readers.setdefault(k, []).append(o)
        for k in writes:
            self.last_w[k] = o
            self.readers[k] = []
        self.ops[eng].append(o)
        return o

    def barrier(self):
        lasts = []
        for e in self.ENGS:
            for o in reversed(self.ops[e]):
                if o.fn is not None and not o.is_dma:
                    lasts.append(o)
                    break
        dl = [d for d in self.dsem_last if d is not None]
        for e in self.ENGS:
            o = Op()
            o.eng = e; o.fn = None; o.is_dma = False
            o.idx = len(self.ops[e]); o.signal = False; o.sigcount = 0
            o.dsem = None; o.dval = 0
            o.deps = [d for d in lasts if d.eng != e] + dl
            for d in o.deps:
                if not d.is_dma:
                    d.signal = True
            self.ops[e].append(o)
        self.last_w = {}
        self.readers = {}

    def emit(self, block, esems, dsems):
        for e in self.ENGS:
            c = 0
            for o in self.ops[e]:
                if o.signal and not o.is_dma:
                    c += 1
                    o.sigcount = c

        def run(ename, eng):
            waited = {}
            for o in self.ops[ename]:
                need = {}
                for d in o.deps:
                    if d.is_dma:
                        key = ("d", d.dsem); val = d.dval
                    else:
                        key = ("e", d.eng); val = d.sigcount
                    if waited.get(key, 0) >= val:
                        continue
                    if need.get(key, 0) < val:
                        need[key] = val
                for key, val in need.items():
                    sem = dsems[key[1]] if key[0] == "d" else esems[key[1]]
                    eng.wait_ge(sem, val)
                    waited[key] = val
                if o.fn is None:
                    continue
                ins = o.fn(eng)
                if o.is_dma:
                    ins.then_inc(dsems[o.dsem], 16)
                elif o.signal:
                    ins.then_inc(esems[ename], 1)

        @block.tensor
        def _(eng):
            run("pe", eng)

        @block.scalar
        def _(eng):
            run("act", eng)

        @block.vector
        def _(eng):
            run("dve", eng)

        @block.gpsimd
        def _(eng):
            run("pool", eng)

        @block.sync
        def _(eng):
            run("sp", eng)


def build(nlayers=4, final_norm=True):
    nc = bass.Bass("TRN2", target_bir_lowering=False)
    dt_in = lambda name, shape, dt: nc.dram_tensor(name, shape, dt, kind="ExternalInput").ap()
    x_in = dt_in("x", [S, D], F32)
    mem_in = dt_in("mem", [256, D], F32)
    pos_in = dt_in("pos", [1, S], I32)
    gains_in = dt_in("gains", [128, 5, 8], F32)
    w_in_e = dt_in("w_in_e", [2, 48, 128, 8, 128], F32)
    w_in_o = dt_in("w_in_o", [2, 32, 128, 8, 128], F32)
    w_kv_e = dt_in("w_kv_e", [2, 8, 128, 8, 128], F32)
    w_kv_o = dt_in("w_kv_o", [2, 8, 128, 8, 128], F32)
    w_out_e = dt_in("w_out_e", [2, 8, 128, 12, 128], F32)
    w_out_o = dt_in("w_out_o", [2, 8, 128, 12, 128], F32)
    w_pool_in = dt_in("w_pool", [2, 128, 4, 128], F32)
    pscale_in = dt_in("pscale", [128, 2, 4], F32)
    lnp_in = dt_in("lnp", [128, 2, 2, 8], F32)
    ws_in = dt_in("ws", [2, 128, 8, 128], F32)
    bs_in = dt_in("bs", [2, 1, 1024], F32)
    fg_in = dt_in("fg", [1, D], F32)
    cbf_in = dt_in("cbf", [128, 128 + 128 + 256 + 256 + 32], BF16)
    cf_in = dt_in("cf", [128, 128 + 64 + 2], F32)
    out = nc.dram_tensor("out", [S, D], F32, kind="ExternalOutput").ap()

    xb = [nc.dram_tensor("xb%d" % i, [S, D], F32).ap() for i in range(2)]
    Vd = nc.dram_tensor("Vd", [S, 512], BF16).ap()
    Yd = nc.dram_tensor("Yd", [12, 128, S], BF16).ap()
    CSd = nc.dram_tensor("CSd", [2, 32, S], F32).ap()

    P = Prog()
    with contextlib.ExitStack() as st:
        esems = {e: st.enter_context(nc.semaphore("es_" + e)) for e in Prog.ENGS}
        dsems = [st.enter_context(nc.semaphore("ds%d" % i)) for i in range(N_DSEM)]
        sb = lambda name, shape, dt: st.enter_context(nc.sbuf_tensor("s_" + name, shape, dt))
        psb = [st.enter_context(nc.psum_tensor("psb%d" % i, [128, 512], F32)) for i in range(8)]

        cbf = sb("cbf", [128, 800], BF16)
        IDENT = cbf[:, 0:128]
        ONES = cbf[:, 128:256]
        MB = cbf[:, 256:512]
        TWOS = cbf[:, 512:640]
        PERM = cbf[:, 768:800]
        cf = sb("cf", [128, 194], F32)
        TRI = cf[:, 0:128]
        INVC = cf[:, 128:192].rearrange("p (g t) -> p g t", g=4)
        INVF = cf[0:32, 192:193]
        SIGN = cf[0:32, 193:194]
        gains = sb("gains", [128, 5, 8], F32)
        memT = sb("memT", [128, 8, 256], BF16)
        hTflat = sb("hT", [128, 8 * S], BF16)
        hT = hTflat[:].rearrange("p (k t) -> p k t", k=8)
        ssn = sb("ssn", [128, 32], F32)
        MKT = sb("MKT", [128, 4, 256], BF16)
        MV = sb("MV", [128, 2, 512], BF16)
        small = sb("small", [128, 64], F32)
        ARN = 65 * 1024
        arena = sb("arena", [128, ARN], BF16)

        class Carve:
            def __init__(self):
                self.off = 0

            def bf(self, n):
                a = arena[:, self.off:self.off + n]
                self.off += n
                assert self.off <= ARN, self.off
                return a

            def f32(self, n):
                return self.bf(2 * n).bitcast(F32)

        block = st.enter_context(nc.Block())

        def dma(out_ap, in_ap, reads=(), writes=(), eng="sp"):
            return P.op(eng, lambda e: e.dma_start(out=out_ap, in_=in_ap), reads, writes, dma=True)

        def mm(out_ap, lhsT, rhs, start, stop, reads, writes, skip=False):
            return P.op("pe", lambda e: e.matmul(out_ap, lhsT=lhsT, rhs=rhs, start=start, stop=stop,
                                                 skip_group_check=skip), reads, writes)

        def act(out_ap, in_ap, func, reads, writes, **kw):
            return P.op("act", lambda e: e.activation(out=out_ap, in_=in_ap, func=func, **kw), reads, writes)

        def tt(eng, out_ap, a, b, op, reads, writes):
            return P.op(eng, lambda e: e.tensor_tensor(out=out_ap, in0=a, in1=b, op=op), reads, writes)

        def ts(eng, out_ap, a, s1, s2, op0, op1, reads, writes):
            if s2 is None:
                return P.op(eng, lambda e: e.tensor_scalar(out=out_ap, in0=a, scalar1=s1, scalar2=None, op0=op0),
                            reads, writes)
            return P.op(eng, lambda e: e.tensor_scalar(out=out_ap, in0=a, scalar1=s1, scalar2=s2, op0=op0, op1=op1),
                        reads, writes)

        def stt(eng, out_ap, a, s, b, op0, op1, reads, writes):
            return P.op(eng, lambda e: e.scalar_tensor_tensor(out=out_ap, in0=a, scalar=s, in1=b, op0=op0, op1=op1),
                        reads, writes)

        def cp(eng, out_ap, in_ap, reads, writes):
            return P.op(eng, lambda e: e.tensor_copy(out=out_ap, in_=in_ap), reads, writes)

        def load_w(src_ap, dst_ap, dkey, gain=None, kc=8):
            dma(dst_ap, src_ap, writes=[dkey], eng="pool")

        class WStream:
            NBUF = 6
            AHEAD = 3

            def __init__(self, items, bufs, gain):
                self.items = items
                self.bufs = bufs
                self.gain = gain
                self.issued = 0
                self.k = 0

            def _issue_to(self, n):
                while self.issued < min(n, len(self.items)):
                    j = self.issued
                    load_w(self.items[j], self.bufs[j % self.NBUF], ("wsb", j % self.NBUF), gain=self.gain)
                    self.issued += 1

            def prefetch(self):
                self._issue_to(self.AHEAD)

            def get(self):
                k = self.k
                self.k += 1
                self._issue_to(k + 1 + self.AHEAD)
                return self.bufs[k % self.NBUF], ("wsb", k % self.NBUF)

        def proj_fm(wb, wkey, stt_, bank):
            for kc in range(8):
                mm(psb[bank][:, :], wb[:, kc, :], hT[:, kc, stt_ * 512:(stt_ + 1) * 512], kc == 0, kc == 7,
                   [wkey] + [("hT", stt_ * 4 + j_) for j_ in range(4)], ["ps%d" % bank])

        def rms_rstd(ss_ap, n, key, extra=()):
            ts("dve", ss_ap, ss_ap, 1.0 / D, EPS, ALU.mult, ALU.add, [key] + list(extra), [key])
            act(ss_ap, ss_ap, AF.Sqrt, [key], [key])
            P.op("dve", lambda e: e.reciprocal(out=ss_ap, in_=ss_ap), [key], [key])

        dma(cbf[:], cbf_in[:, :], writes=["cbf"])
        dma(cf[:], cf_in[:, :], writes=["cf"])
        dma(gains[:], gains_in[:, :, :], writes=["gains"])
        P.barrier()

        cv = Carve()
        cv.off = 27 * 1024
        HS = S // 2
        posi = cv.bf(2 * HS).bitcast(I32)
        ang = cv.f32(HS)
        yv = cv.f32(HS)
        ki = cv.bf(2 * HS).bitcast(I32)
        kf = cv.f32(HS)
        tab = cv.f32(HS)
        a32 = lambda t: t[0:32, :]
        rope_thunks = []
        RT = rope_thunks.append
        for hh in range(2):
            RT(lambda hh=hh: dma(posi[0:32, :], pos_in[0:1, hh * HS:(hh + 1) * HS].broadcast_to([32, HS]), writes=["posi"]))
            RT(lambda: cp("dve", ang[0:32, :], posi[0:32, :], ["posi"], ["ang"]))
            RT(lambda: ts("dve", ang[0:32, :], ang[0:32, :], INVF, None, ALU.mult, None, ["ang"], ["ang"]))
            for ti, shift in enumerate((0.75, 0.5)):
                RT(lambda shift=shift: ts("dve", a32(yv), a32(ang), float(1.0 / (2 * np.pi)), shift, ALU.mult, ALU.add, ["ang"], ["yv"]))
                RT(lambda: cp("dve", a32(ki), a32(yv), ["yv"], ["ki"]))
                RT(lambda: cp("dve", a32(kf), a32(ki), ["ki"], ["kf"]))
                RT(lambda: tt("dve", a32(yv), a32(yv), a32(kf), ALU.subtract, ["yv", "kf"], ["yv"]))
                RT(lambda: ts("dve", a32(kf), a32(yv), 0.0, None, ALU.is_lt, None, ["yv"], ["kf"]))
                RT(lambda: tt("dve", a32(yv), a32(yv), a32(kf), ALU.add, ["yv", "kf"], ["yv"]))
                RT(lambda: ts("dve", a32(yv), a32(yv), float(2 * np.pi), -float(np.pi), ALU.mult, ALU.add, ["yv"], ["yv"]))
                if ti == 0:
                    RT(lambda: act(a32(tab), a32(yv), AF.Sin, ["yv"], ["tab"]))
                else:
                    RT(lambda: act(a32(tab), a32(yv), AF.Sin, ["yv"], ["tab"], scale=SIGN))
                RT(lambda ti=ti, hh=hh: dma(CSd[ti, :, hh * HS:(hh + 1) * HS], a32(tab), reads=["tab"], writes=[("CSd", ti, hh)]))

        mx = [cv.f32(D) for _ in range(1)] * 2
        mh = [cv.bf(D) for _ in range(1)] * 2
        assert cv.off <= ARN - 9216
        for mb_ in range(2):
            dma(mx[mb_], mem_in[mb_ * 128:(mb_ + 1) * 128, :], writes=["mx0"])
            ssm = small[:, mb_:mb_ + 1]
            P.op("dve", lambda e, ssm=ssm: e.memset(ssm, 0.0), [], ["ssm%d" % mb_])
            act(mh[mb_], mx[mb_], AF.Square, ["mx0"], ["mh0", "ssm%d" % mb_], accum_out=ssm)
            rms_rstd(ssm, 1, "ssm%d" % mb_)
            ts("dve", mh[mb_], mx[mb_], ssm, None, ALU.mult, None, ["mx0", "ssm%d" % mb_], ["mh0"])
            b = P.bank()
            pt = psb[b][:].bitcast(BF16).rearrange("p (k t) -> p k t", k=8)
            for kc in range(8):
                P.op("pe", lambda e, kc=kc, pt=pt, mb_=mb_: e.transpose(out=pt[:, kc, :], in_=mh[mb_][:, kc * 128:(kc + 1) * 128],
                                                                    identity=IDENT),
                     ["mh0"], ["ps%d" % b])
            tt("dve", memT[:, :, mb_ * 128:(mb_ + 1) * 128], pt, gains[:, 0, :].unsqueeze(2).broadcast_to([128, 8, 128]), ALU.mult,
               ["ps%d" % b], ["memT"])

        def tail_carve(n):
            c = Carve()
            c.off = ARN - n
            return c

        def pass1(xsrc, xkey, gidx, have_stats, do_barrier=True):
            cv = tail_carve(9216)
            xt = [cv.f32(D) for _ in range(3)]
            hb = [cv.bf(D) for _ in range(3)]
            if not have_stats:
                P.op("dve", lambda e: e.memset(ssn[:], 0.0), [], ["SSN"] + [("ssn", t_) for t_ in range(NTB)])
                for tb in range(NTB):
                    s_ = tb % 3
                    dma(xt[s_], xsrc[tb * 128:(tb + 1) * 128, :], reads=[(xkey, tb)], writes=["xt%d" % s_])
                    act(hb[s_], xt[s_], AF.Square, ["xt%d" % s_], ["hb%d" % s_, ("ssn", tb)], accum_out=ssn[:, tb:tb + 1])
                rms_rstd(ssn[:], 32, "SSN", extra=[("ssn", t_) for t_ in range(NTB)])
            gb = gains[:, gidx, :].unsqueeze(2).broadcast_to([128, 8, 128])
            for tb in range(NTB):
                s_ = tb % 3
                dma(xt[s_], xsrc[tb * 128:(tb + 1) * 128, :], reads=[(xkey, tb)], writes=["xt%d" % s_])
                act(hb[s_], xt[s_], AF.Copy, ["xt%d" % s_, "SSN"], ["hb%d" % s_], scale=ssn[:, tb:tb + 1])
                b = P.bank()
                pt = psb[b][:].bitcast(BF16).rearrange("p (k t) -> p k t", k=8)
                for kc in range(8):
                    P.op("pe", lambda e, kc=kc, pt=pt, s_=s_: e.transpose(out=pt[:, kc, :], in_=hb[s_][:, kc * 128:(kc + 1) * 128],
                                                                     identity=IDENT),
                         ["hb%d" % s_], ["ps%d" % b])
                tt("dve", hT[:, :, tb * 128:(tb + 1) * 128], pt, gb, ALU.mult, ["ps%d" % b], [("hT", tb)])
            if do_barrier:
                P.barrier()

        def mem_kv(wbt):
            for f in range(8):
                s_ = f
                if f < 4:
                    b = P.bank()
                    for kc in range(8):
                        mm(psb[b][:, 0:256], wbt[s_][:, kc, :], memT[:, kc, :], kc == 0, kc == 7, ["wbm%d" % s_], ["ps%d" % b])
                    cp("dve", MKT[:, f, :], psb[b][:, 0:256], ["ps%d" % b], ["MKT"])
                else:
                    b = P.bank()
                    for mb_ in range(2):
                        for kc in range(8):
                            mm(psb[b][:, mb_ * 128:(mb_ + 1) * 128], memT[:, kc, mb_ * 128:(mb_ + 1) * 128], wbt[s_][:, kc, :],
                               mb_ == 0 and kc == 0, kc == 7, ["wbm%d" % s_], ["ps%d" % b], skip=True)
                    cp("dve", MV[:, :, (f - 4) * 128:(f - 3) * 128], psb[b][:, 0:256].rearrange("p (m c) -> p m c", m=2),
                       ["ps%d" % b], ["MV"])

        def gate_from_z(wz, wzkey, stt_, TH, G, gslot):
            b = P.bank()
            proj_fm(wz, wzkey, stt_, b)
            act(TH[gslot], psb[b][:, :], AF.Tanh, ["ps%d" % b], ["TH%d" % gslot], scale=0.5)
            stt("dve", G[gslot], TH[gslot], 1.0, psb[b][:, :], ALU.add, ALU.mult, ["TH%d" % gslot, "ps%d" % b], ["G%d" % gslot])

        def mem_attn(wsm, yi0, bufs):
            QM, PM, RD, TH, G, YO = bufs
            N = 4 * NST
            wts = {}
            stA, stB = {}, {}

            def A(n):
                h, stt_ = divmod(n, NST)
                if stt_ == 0:
                    wts[h] = (wsm.get(), wsm.get())
                (wq, wqk), (wz, wzk) = wts[h]
                sl = n % 2
                b = P.bank()
                proj_fm(wq, wqk, stt_, b)
                act(QM[sl], psb[b][:, :], AF.Copy, ["ps%d" % b], ["QM%d" % sl])
                gate_from_z(wz, wzk, stt_, TH, G, sl)

            def B(n):
                h, stt_ = divmod(n, NST)
                sl = n % 2
                bs_ = [P.bank(), P.bank()]
                for mb_ in range(2):
                    mm(psb[bs_[mb_]][:, :], MKT[:, h, mb_ * 128:(mb_ + 1) * 128], QM[sl], True, True,
                       ["MKT", "QM%d" % sl], ["ps%d" % bs_[mb_]])
                    act(PM[sl][:, mb_, :], psb[bs_[mb_]][:, :], AF.Exp, ["ps%d" % bs_[mb_]], ["PM%d_%d" % (sl, mb_)], scale=SCALE)

            def C(n):
                h, stt_ = divmod(n, NST)
                sl = n % 2
                bn, bd = P.bank(), P.bank()
                for mb_ in range(2):
                    mm(psb[bn][:, :], MV[:, mb_, h * 128:(h + 1) * 128], PM[sl][:, mb_, :], mb_ == 0, mb_ == 1,
                       ["MV", "PM%d_%d" % (sl, mb_)], ["ps%d" % bn])
                for mb_ in range(2):
                    mm(psb[bd][:, :], TWOS, PM[sl][:, mb_, :], mb_ == 0, mb_ == 1,
                       ["PM%d_%d" % (sl, mb_)], ["ps%d" % bd])
                P.op("dve", lambda e, sl=sl, bd=bd: e.reciprocal(out=RD[sl], in_=psb[bd][:, :]), ["ps%d" % bd], ["RD%d" % sl])
                tt("dve", RD[sl], psb[bn][:, :], RD[sl], ALU.mult, ["ps%d" % bn, "RD%d" % sl], ["RD%d" % sl])
                tt("dve", YO[sl], RD[sl], G[sl], ALU.mult, ["RD%d" % sl, "G%d" % sl], ["YO%d" % sl])
                dma(Yd[yi0 + h, :, stt_ * 512:(stt_ + 1) * 512], YO[sl], reads=["YO%d" % sl], writes=[("Yd", yi0 + h, stt_)])

            A(0)
            B(0)
            A(1)
            for n in range(N):
                C(n)
                if n + 1 < N:
                    B(n + 1)
                if n + 2 < N:
                    A(n + 2)

        def pass3(xsrc, xkey, WO, xdst, dkey, last):
            YT = [hTflat[:, 12288 + i * 6144:12288 + (i + 1) * 6144].rearrange("p (k t) -> p k t", k=12) for i in range(2)]
            XT = [hTflat[:, 24576 + i * 2048:24576 + (i + 1) * 2048].bitcast(F32) for i in range(2)]
            XO = [hTflat[:, 28672 + i * 2048:28672 + (i + 1) * 2048].bitcast(F32) for i in range(2)]
            cv = tail_carve(4096)
            FG = cv.f32(D)
            JUNK = cv.f32(D)
            if last:
                dma(FG, fg_in[0:1, :].broadcast_to([128, D]), writes=["FG"])
            else:
                P.op("dve", lambda e: e.memset(ssn[:], 0.0), [], ["SSN"] + [("ssn", t_) for t_ in range(NTB)])
            def ld_y(stt_):
                ys = stt_ % 2
                dma(YT[ys], Yd[:, :, stt_ * 512:(stt_ + 1) * 512].rearrange("k p t -> p k t"),
                    reads=[("Yd", k, stt_) for k in range(12)], writes=["YT%d" % ys])

            def ld_x(tb):
                dma(XT[tb % 2], xsrc[tb * 128:(tb + 1) * 128, :], reads=[(xkey, tb)], writes=["XT%d" % (tb % 2)])

            ld_y(0)
            ld_x(0)
            for stt_ in range(NST):
                ys = stt_ % 2
                for t4 in range(4):
                    tb = stt_ * 4 + t4
                    s_ = tb % 2
                    if t4 == 1 and stt_ + 1 < NST:
                        ld_y(stt_ + 1)
                    if tb + 1 < NTB:
                        ld_x(tb + 1)
                    for half in range(2):
                        b = P.bank()
                        for k in range(12):
                            mm(psb[b][:, :], YT[ys][:, k, t4 * 128:(t4 + 1) * 128], WO[:, k, half * 512:(half + 1) * 512],
                               k == 0, k == 11, ["YT%d" % ys] + [("WO", n) for n in range(half * 4, half * 4 + 4)], ["ps%d" % b])
                        tt("dve", XO[s_][:, half * 512:(half + 1) * 512], XT[s_][:, half * 512:(half + 1) * 512], psb[b][:, :],
                           ALU.add, ["XT%d" % s_, "ps%d" % b], [("XO", s_, half)])
                    if last and final_norm:
                        ss = small[:, 16 + s_:17 + s_]
                        P.op("dve", lambda e, ss=ss: e.memset(ss, 0.0), [], ["ssf%d" % s_])
                        act(JUNK, XO[s_], AF.Square, [("XO", s_, 0), ("XO", s_, 1)], ["JUNK", "ssf%d" % s_], accum_out=ss)
                        rms_rstd(ss, 1, "ssf%d" % s_)
                        stt("dve", XO[s_], XO[s_], ss, FG, ALU.mult, ALU.mult, [("XO", s_, 0), ("XO", s_, 1), "ssf%d" % s_, "FG"],
                            [("XO", s_, 0), ("XO", s_, 1)])
                    dma(xdst[tb * 128:(tb + 1) * 128, :], XO[s_], reads=[("XO", s_, 0), ("XO", s_, 1)], writes=[(dkey, tb)])
                    if not last:
                        act(JUNK, XO[s_], AF.Square, [("XO", s_, 0), ("XO", s_, 1)], ["JUNK", ("ssn", tb)],
                            accum_out=ssn[:, tb:tb + 1])
            if not last:
                rms_rstd(ssn[:], 32, "SSN", extra=[("ssn", t_) for t_ in range(NTB)])
            P.barrier()

        def even_pre(i):
            gain = None
            win = w_in_e[i]
            cv = Carve()
            wsb = [cv.bf(1024).rearrange("p (k n) -> p k n", k=8) for _ in range(6)]
            items = []

            def qk_items(u):
                hh, gg = divmod(u, 3)
                return [win[(gg * 2 + 1) * 4 + hh], win[(gg * 2) * 4 + hh]]

            items += qk_items(0)
            for u in range(12):
                if u + 1 < 12:
                    items += qk_items(u + 1)
                if u % 3 == 2:
                    items.append(win[36 + u // 3])
            for gi in range(4):
                items += [win[28 + gi], win[40 + gi]]
            for h in range(4):
                items += [win[32 + h], win[44 + h]]
            wse = WStream(items, wsb, gain)
            TH = [cv.f32(512) for _ in range(2)]
            G = [cv.f32(512) for _ in range(2)]
            RD = [cv.f32(512) for _ in range(2)]
            YO = [cv.bf(512) for _ in range(2)]
            mark = cv.off
            wbt = [cv.bf(1024).rearrange("p (k n) -> p k n", k=8) for _ in range(8)]
            WV = cv.bf(4096).rearrange("p (k n) -> p k n", k=8)
            vst = [cv.bf(512) for _ in range(2)]
            for f in range(8):
                load_w(w_kv_e[i][f], wbt[f], "wbm%d" % f)
            for j in range(4):
                load_w(win[24 + j], WV[:, :, j * 128:(j + 1) * 128], ("WV", j))
            wse.prefetch()
            return (cv, wse, TH, G, RD, YO, mark, wbt, WV, vst)

        def even_pass2(i, state):
            cv, wse, TH, G, RD, YO, mark, wbt, WV, vst = state
            win = w_in_e[i]
            mem_kv(wbt)

            for tb in range(NTB):
                s_ = tb % 2
                b = P.bank()
                for kc in range(8):
                    mm(psb[b][:, :], hT[:, kc, tb * 128:(tb + 1) * 128], WV[:, kc, :], kc == 0, kc == 7,
                       [("WV", j) for j in range(4)] + [("hT", tb)], ["ps%d" % b])
                if tb % 2 == 0:
                    act(vst[s_], psb[b][:, :], AF.Copy, ["ps%d" % b], ["vst%d" % s_])
                else:
                    cp("dve", vst[s_], psb[b][:, :], ["ps%d" % b], ["vst%d" % s_])
                dma(Vd[tb * 128:(tb + 1) * 128, :], vst[s_], reads=["vst%d" % s_], writes=[("Vd", tb)])
                for _ in range(2):
                    if rope_thunks:
                        rope_thunks.pop(0)()
            while rope_thunks:
                rope_thunks.pop(0)()

            P.barrier()
            cv.off = mark
            VG = [cv.bf(4096).rearrange("p (j c) -> p j c", j=32) for _ in range(2)]
            KT = [cv.bf(S) for _ in range(2)]
            QT = [cv.bf(S) for _ in range(2)]
            ACCN = cv.f32(S)
            ACCD = cv.f32(S)
            CS = [cv.f32(1024).rearrange("p (a t) -> p a t", a=2) for _ in range(2)]
            TMPN = [cv.bf(512) for _ in range(4)]
            T1 = [cv.f32(512) for _ in range(2)]
            T2 = [cv.f32(512) for _ in range(2)]
            PS_ = [cv.bf(512) for _ in range(4)]
            psctr = [0]
            ropectr = [0]
            kqctr = [0]

            def load_vg(u):
                hh, gg = divmod(u, 3)
                dl = DILS[gg]
                src = Vd[:, hh * 128:(hh + 1) * 128].rearrange("(jj n r) c -> n r jj c", n=128, r=dl)
                dst = VG[u % 2].rearrange("n (r jj) c -> n r jj c", r=dl)
                rd = [("Vd", tb) for tb in range(NTB)]
                for r in range(dl):
                    dma(dst[:, r, :, :], src[:, r, :, :], reads=rd, writes=[("VG", u % 2, r)])

            def QK(unit):
                h, g = divmod(unit, 3)
                dil = DILS[g]
                kb = unit % 2
                wk, wkk = wse.get()
                wq, wqk = wse.get()
                m_ = 512 // dil
                KD, QD = KT[kb], QT[kb]

                def QK_proj(stt_):
                    info = []
                    for which, (wt, wkey, DST, dname) in enumerate(((wk, wkk, KD, "KT"), (wq, wqk, QD, "QT"))):
                        b = P.bank()
                        proj_fm(wt, wkey, stt_, b)
                        tn = (stt_ % 2) * 2 + which
                        act(TMPN[tn], psb[b][:, :], AF.Copy, ["ps%d" % b], ["TMPN%d" % tn])
                        dv = DST.rearrange("p (r s) -> p r s", r=dil)[:, :, stt_ * m_:(stt_ + 1) * m_]
                        sv = psb[b][:, :].rearrange("p (m r) -> p r m", r=dil)
                        act(dv, sv, AF.Copy, ["ps%d" % b], [(dname, kb, stt_, "a")])
                        info.append((b, tn, dv, dname))
                    return info

                def QK_rope(stt_, info):
                    cs = ropectr[0] % 2
                    ropectr[0] += 1
                    dma(CS[cs][0:32, :, :], CSd[:, :, stt_ * 512:(stt_ + 1) * 512].rearrange("a p t -> p a t"),
                        reads=[("CSd", a_, h_) for a_ in range(2) for h_ in range(2)], writes=["CS%d" % cs])
                    for which, (b, tn, dv, dname) in enumerate(info):
                        b2 = P.bank()
                        mm(psb[b2][0:32, :], PERM, TMPN[tn], True, True, ["TMPN%d" % tn], ["ps%d" % b2])
                        tt("dve", T1[which][0:32, :], psb[b][0:32, :], CS[cs][0:32, 0, :], ALU.mult,
                           ["ps%d" % b, "CS%d" % cs], ["T1%d" % which])
                        tt("dve", T2[which][0:32, :], psb[b2][0:32, :], CS[cs][0:32, 1, :], ALU.mult,
                           ["ps%d" % b2, "CS%d" % cs], ["T2%d" % which])
                        tt("pool", dv[0:32], T1[which][0:32, :].rearrange("p (m r) -> p r m", r=dil),
                           T2[which][0:32, :].rearrange("p (m r) -> p r m", r=dil), ALU.add,
                           ["T1%d" % which, "T2%d" % which, (dname, kb, stt_, "a")], [(dname, kb, stt_, "b")])

                infos = {}
                for stt_ in range(NST):
                    infos[stt_] = QK_proj(stt_)
                    if stt_ > 0:
                        QK_rope(stt_ - 1, infos[stt_ - 1])
                QK_rope(NST - 1, infos[NST - 1])

            def ATT(unit):
                h, g = divmod(unit, 3)
                dil = DILS[g]
                nbk = 32 // dil
                kb = unit % 2
                VGu = VG[unit % 2]
                kkeys = [("KT", kb, s2, ab) for s2 in range(NST) for ab in "ab"]
                qkeys = [("QT", kb, s2, ab) for s2 in range(NST) for ab in "ab"]
                vkeys = [("VG", unit % 2, r) for r in range(dil)]
                abanks = {}

                def AS(jj):
                    sbanks = [P.bank(), P.bank()]
                    abanks[jj] = sbanks
                    for jl in range(4):
                        j = jj * 4 + jl
                        has_prev = (j % nbk) != 0
                        sbk = sbanks[jl // 2]
                        so = (jl % 2) * 256
                        first = (jl % 2 == 0)
                        qsl = QT[kb][:, j * 128:(j + 1) * 128]
                        if has_prev:
                            mm(psb[sbk][:, so:so + 256], IDENT, MB, first, False, ["cbf"], ["ps%d" % sbk], skip=True)
                            mm(psb[sbk][:, so:so + 128], KT[kb][:, (j - 1) * 128:j * 128], qsl,
                               False, False, kkeys + qkeys, ["ps%d" % sbk], skip=True)
                        else:
                            mm(psb[sbk][:, so + 128:so + 256], IDENT, MB[:, 128:256], first, False, ["cbf"], ["ps%d" % sbk], skip=True)
                        mm(psb[sbk][:, so + 128:so + 256], KT[kb][:, j * 128:(j + 1) * 128], qsl,
                           False, True, kkeys + qkeys, ["ps%d" % sbk], skip=True)

                def AV(jj):
                    sbanks = abanks.pop(jj)
                    bn, bd = P.bank(), P.bank()
                    for half in range(2):
                        pslot = psctr[0] % 4
                        psctr[0] += 1
                        pk = "PS%d" % pslot
                        PSx = PS_[pslot]
                        act(PSx, psb[sbanks[half]][:, :], AF.Exp, ["ps%d" % sbanks[half]], [pk], scale=SCALE)
                        for jl2 in range(2):
                            jl = half * 2 + jl2
                            j = jj * 4 + jl
                            has_prev = (j % nbk) != 0
                            so = jl2 * 256
                            first = (jl == 0)
                            last_ = (jl == 3)
                            if has_prev:
                                mm(psb[bn][:, jl * 128:(jl + 1) * 128], VGu[:, j - 1, :], PSx[:, so:so + 128], first, False,
                                   vkeys + [pk], ["ps%d" % bn], skip=True)
                            mm(psb[bn][:, jl * 128:(jl + 1) * 128], VGu[:, j, :], PSx[:, so + 128:so + 256],
                               first and not has_prev, last_, vkeys + [pk], ["ps%d" % bn], skip=True)
                            if has_prev:
                                mm(psb[bd][:, jl * 128:(jl + 1) * 128], TWOS, PSx[:, so:so + 128], first, False,
                                   [pk], ["ps%d" % bd], skip=True)
                            mm(psb[bd][:, jl * 128:(jl + 1) * 128], TWOS, PSx[:, so + 128:so + 256],
                               first and not has_prev, last_, [pk], ["ps%d" % bd], skip=True)
                    spr = S // dil
                    run = min(512, spr)
                    for q in range(512 // run):
                        sig0 = jj * 512 + q * run
                        r = sig0 // spr
                        s0 = sig0 % spr
                        for (ACC, bnk, key) in ((ACCN, bn, "ACCN"), (ACCD, bd, "ACCD")):
                            dstv = ACC.rearrange("p (s r) -> p r s", r=dil)[:, r, s0:s0 + run]
                            srcv = psb[bnk][:, q * run:(q + 1) * run]
                            if g == 0:
                                cp("dve", dstv, srcv, ["ps%d" % bnk], [key])
                            else:
                                tt("dve", dstv, dstv, srcv, ALU.add, ["ps%d" % bnk, key], [key])

                AS(0)
                for jj in range(8):
                    if jj + 1 < 8:
                        AS(jj + 1)
                    AV(jj)

            def FIN(h):
                wz, wzk = wse.get()
                for stt_ in range(NST):
                    sl = stt_ % 2
                    c0, c1 = stt_ * 512, (stt_ + 1) * 512
                    P.op("dve", lambda e, sl=sl, c0=c0, c1=c1: e.reciprocal(out=RD[sl], in_=ACCD[:, c0:c1]), ["ACCD"], ["RD%d" % sl])
                    tt("pool", RD[sl], ACCN[:, c0:c1], RD[sl], ALU.mult, ["ACCN", "RD%d" % sl], ["RD%d" % sl])
                    gate_from_z(wz, wzk, stt_, TH, G, sl)
                    tt("pool", YO[sl], RD[sl], G[sl], ALU.mult, ["RD%d" % sl, "G%d" % sl], ["YO%d" % sl])
                    dma(Yd[h, :, c0:c1], YO[sl], reads=["YO%d" % sl], writes=[("Yd", h, stt_)])

            load_vg(0)
            QK(0)
            for unit in range(12):
                if unit + 1 < 12:
                    QK(unit + 1)
                    load_vg(unit + 1)
                ATT(unit)
                if unit % 3 == 2:
                    FIN(unit // 3)

            P.barrier()
            cv.off = mark
            QM = [cv.bf(512) for _ in range(2)]
            PM = [cv.bf(1024).rearrange("p (m t) -> p m t", m=2) for _ in range(2)]
            WP = cv.bf(512).rearrange("p (g d) -> p g d", g=4)
            dma(WP, w_pool_in[i], writes=["WP"], eng="pool")
            WO = cv.bf(12288).rearrange("p (k n) -> p k n", k=12)
            for n in range(8):
                load_w(w_out_e[i][n], WO[:, :, n * 128:(n + 1) * 128], ("WO", n), kc=12)
            PSC = cv.f32(4)
            dma(PSC, pscale_in[:, i, :], writes=["PSC"])
            ts("dve", PSC, PSC, 0.5, None, ALU.mult, None, ["PSC"], ["PSC"])
            XB = cv.f32(528)
            LV = [cv.f32(528) for _ in range(4)]
            PL = [cv.bf(512) for _ in range(2)]
            T16 = cv.f32(16)
            pw = {}
            pb2 = {}

            def PA(n):
                gi, stt_ = divmod(n, NST)
                L = gi + 1
                w_ = 2 ** L
                if stt_ == 0:
                    pw[gi] = (wse.get(), wse.get())
                    P.op("dve", lambda e: e.memset(XB[:, 0:16], 0.0), [], ["XB"])
                (wq, wqk), (wz, wzk) = pw[gi]
                sl = n % 2
                b = P.bank()
                proj_fm(wq, wqk, stt_, b)
                if stt_ > 0:
                    cp("dve", XB[:, 0:16], XB[:, 512:528], ["XB"], ["XB"])
                act(XB[:, 16:528], psb[b][:, :], AF.Copy, ["ps%d" % b], ["XB"])
                prev = XB
                for l in range(1, L + 1):
                    lo = 16 - (w_ - 2 ** l)
                    sh = 2 ** (l - 1)
                    tt("dve", LV[l - 1][:, lo:528], prev[:, lo:528], prev[:, lo - sh:528 - sh], ALU.add,
                       ["XB", "LV"], ["LV"])
                    prev = LV[l - 1]
                stt("dve", PL[sl], prev[:, 16:528], 1.0 / w_, XB[:, 16:528], ALU.mult, ALU.subtract, ["LV", "XB"], ["PL%d" % sl])
                if stt_ == 0:
                    tt("dve", T16, prev[:, 16:32], INVC[:, gi, :], ALU.mult, ["LV"], ["T16"])
                    tt("dve", PL[sl][:, 0:16], T16, XB[:, 16:32], ALU.subtract, ["T16", "XB"], ["PL%d" % sl])
                gate_from_z(wz, wzk, stt_, TH, G, sl)

            def PB(n):
                gi, stt_ = divmod(n, NST)
                sl = n % 2
                c0, c1 = stt_ * 512, (stt_ + 1) * 512
                b2 = P.bank()
                mm(psb[b2][:, :], WP[:, gi, :], PL[sl], True, True, ["WP", "PL%d" % sl], ["ps%d" % b2])
                stt("dve", YO[sl], psb[b2][:, :], PSC[:, gi:gi + 1], G[sl], ALU.mult, ALU.mult,
                    ["ps%d" % b2, "PSC", "G%d" % sl], ["YO%d" % sl])
                dma(Yd[4 + gi, :, c0:c1], YO[sl], reads=["YO%d" % sl], writes=[("Yd", 4 + gi, stt_)])

            NP_ = 4 * NST
            PA(0)
            for n in range(NP_):
                if n + 1 < NP_:
                    PA(n + 1)
                PB(n)

            mem_attn(wse, 8, (QM, PM, RD, TH, G, YO))
            P.barrier()
            return WO

        def odd_pre(i):
            gain = None
            win = w_in_o[i]
            cv = Carve()
            wvoff = cv.off
            WV = cv.bf(8192).rearrange("p (k n) -> p k n", k=8)
            WU = [cv.bf(1024).rearrange("p (k n) -> p k n", k=8) for _ in range(8)]
            WZ = [cv.bf(1024).rearrange("p (k n) -> p k n", k=8) for _ in range(8)]
            wbtoff = cv.off
            wbt = [cv.bf(1024).rearrange("p (k n) -> p k n", k=8) for _ in range(8)]
            cv.off = wbtoff
            for f in range(8):
                load_w(w_kv_o[i][f], wbt[f], "wbm%d" % f)
            for g in range(8):
                load_w(win[8 + g], WV[:, :, g * 128:(g + 1) * 128], ("WV", g))
            return (cv, wvoff, WV, WU, WZ, wbt)

        def odd_pass2(i, state):
            cv, wvoff, WV, WU, WZ, wbt = state
            gain = None
            win = w_in_o[i]
            mem_kv(wbt)
            P.barrier()
            for g in range(8):
                load_w(win[g], WU[g], ("WU", g))
                load_w(win[20 + g], WZ[g], ("WZ", g))
            WS = cv.bf(1024).rearrange("p (g t) -> p g t", g=8)
            wsst = cv.f32(1024).rearrange("p (g t) -> p g t", g=8)
            dma(wsst, ws_in[i], writes=["wsst"])
            tt("dve", WS, wsst, TRI.unsqueeze(1).broadcast_to([128, 8, 128]), ALU.mult, ["wsst"], ["WS"])
            LNP = cv.f32(16).rearrange("p (a g) -> p a g", a=2)
            dma(LNP, lnp_in[:, i, :, :], writes=["LNP"])
            BSB = cv.f32(1024)
            dma(BSB, bs_in[i, 0:1, :].broadcast_to([128, 1024]), writes=["BSB"])
            BIAS2 = cv.f32(1024).rearrange("p (g t) -> p g t", g=8)
            for hf in range(2):
                b = P.bank()
                mm(psb[b][:, :], ONES, WS[:, hf * 4:(hf + 1) * 4, :], True, True, ["WS"], ["ps%d" % b])
                for g4 in range(4):
                    g = hf * 4 + g4
                    stt("dve", BIAS2[:, g, :], psb[b][:, g4 * 128:(g4 + 1) * 128], LNP[:, 1, g:g + 1], BSB[:, g * 128:(g + 1) * 128],
                        ALU.mult, ALU.add, ["ps%d" % b, "LNP", "BSB"], ["BIAS2"])
            VH = [cv.bf(4096).rearrange("p (c f) -> p c f", c=4) for _ in range(2)]
            UT = [cv.bf(512) for _ in range(2)]
            M1 = [cv.f32(512) for _ in range(2)]
            TH = [cv.f32(512) for _ in range(2)]
            G = [cv.f32(512) for _ in range(2)]
            RD = [cv.f32(512) for _ in range(2)]
            YO = [cv.bf(512) for _ in range(2)]
            QM = [cv.bf(512) for _ in range(2)]
            PM = [cv.bf(1024).rearrange("p (m t) -> p m t", m=2) for _ in range(2)]
            wsb = [cv.bf(1024).rearrange("p (k n) -> p k n", k=8) for _ in range(6)]
            items = []
            for h in range(4):
                items += [win[16 + h], win[28 + h]]
            wso = WStream(items, wsb, gain)
            wso.prefetch()
            STATS = [cv.f32(48).rearrange("p (c a s) -> p c a s", c=4, a=2) for _ in range(2)]
            MVR = [cv.f32(8).rearrange("p (c a) -> p c a", c=4) for _ in range(2)]
            RS4 = [cv.f32(4) for _ in range(2)]
            wvkeys = [("WV", g) for g in range(8)]

            def VCH(st_, c):
                vs = st_ % 2
                tb = st_ * 4 + c
                bb = (P.bank(), P.bank())
                for hf, b in enumerate(bb):
                    for kc in range(8):
                        mm(psb[b][:, :], hT[:, kc, tb * 128:(tb + 1) * 128], WV[:, kc, hf * 512:(hf + 1) * 512], kc == 0, kc == 7,
                           wvkeys + [("hT", tb)], ["ps%d" % b])
                    P.op("dve", lambda e, vs=vs, c=c, hf=hf, b=b: e.bn_stats(out=STATS[vs][:, c, hf, :], in_=psb[b][:, :]),
                         ["ps%d" % b], [("STATS", vs, c, hf)])
                P.op("dve", lambda e, vs=vs, c=c: e.bn_aggr(out=MVR[vs][:, c, :], in_=STATS[vs][:, c, :, :]),
                     [("STATS", vs, c, 0), ("STATS", vs, c, 1)], [("MVR", vs, c)])
                rs = RS4[vs][:, c:c + 1]
                ts("dve", rs, MVR[vs][:, c, 1:2], EPS, None, ALU.add, None, [("MVR", vs, c)], [("RS4", vs, c)])
                act(rs, rs, AF.Sqrt, [("RS4", vs, c)], [("RS4", vs, c)])
                P.op("dve", lambda e, rs=rs: e.reciprocal(out=rs, in_=rs), [("RS4", vs, c)], [("RS4", vs, c)])
                for hf, b in enumerate(bb):
                    ts("dve", VH[vs][:, c, hf * 512:(hf + 1) * 512], psb[b][:, :], MVR[vs][:, c, 0:1], rs,
                       ALU.subtract, ALU.mult, ["ps%d" % b, ("MVR", vs, c), ("RS4", vs, c)], [("VH", vs, c, hf)])

            for c in range(4):
                VCH(0, c)
            for stt_ in range(NST):
                vs = stt_ % 2
                for g in range(8):
                    sl = g % 2
                    c0, c1 = stt_ * 512, (stt_ + 1) * 512
                    b = P.bank()
                    proj_fm(WU[g], ("WU", g), stt_, b)
                    act(UT[sl], psb[b][:, :], AF.Copy, ["ps%d" % b], ["UT%d" % sl])
                    bm = P.bank()
                    for c in range(4):
                        mm(psb[bm][:, c * 128:(c + 1) * 128], VH[vs][:, c, g * 128:(g + 1) * 128], WS[:, g, :], c == 0, c == 3,
                           [("VH", vs, c, g // 4), "WS"], ["ps%d" % bm], skip=True)
                    stt("dve", M1[sl].rearrange("p (c t) -> p c t", c=4), psb[bm][:, :].rearrange("p (c t) -> p c t", c=4),
                        LNP[:, 0, g:g + 1], BIAS2[:, g, :].unsqueeze(1).broadcast_to([128, 4, 128]), ALU.mult, ALU.add,
                        ["ps%d" % bm, "LNP", "BIAS2"], ["M1%d" % sl])
                    gate_from_z(WZ[g], ("WZ", g), stt_, TH, G, sl)
                    stt("dve", M1[sl], M1[sl], 0.5, UT[sl], ALU.mult, ALU.mult, ["M1%d" % sl, "UT%d" % sl], ["M1%d" % sl])
                    tt("pool", YO[sl], M1[sl], G[sl], ALU.mult, ["M1%d" % sl, "G%d" % sl], ["YO%d" % sl])
                    dma(Yd[g, :, c0:c1], YO[sl], reads=["YO%d" % sl], writes=[("Yd", g, stt_)])
                    if g % 2 == 1 and stt_ + 1 < NST:
                        VCH(stt_ + 1, g // 2)
            P.barrier()
            cw = Carve()
            cw.off = wvoff
            WO = cw.bf(12288).rearrange("p (k n) -> p k n", k=12)
            assert cw.off <= wvoff + 8192 + 8192
            for n in range(8):
                load_w(w_out_o[i][n], WO[:, :, n * 128:(n + 1) * 128], ("WO", n), kc=12)
            mem_attn(wso, 8, (QM, PM, RD, TH, G, YO))
            P.barrier()
            return WO

        for L in range(nlayers):
            i = L // 2
            xsrc, xkey = (x_in, "xin") if L == 0 else (xb[(L - 1) % 2], "xb%d" % ((L - 1) % 2))
            last = (L == nlayers - 1)
            xdst, dkey = (out, "out") if last else (xb[L % 2], "xb%d" % (L % 2))
            if L % 2 == 0:
                state = even_pre(i)
                pass1(xsrc, xkey, 1 + i, L > 0, do_barrier=False)
                WO = even_pass2(i, state)
            else:
                state = odd_pre(i)
                pass1(xsrc, xkey, 3 + i, L > 0)
                WO = odd_pass2(i, state)
            pass3(xsrc, xkey, WO, xdst, dkey, last)
        P.emit(block, esems, dsems)
    return nc


def _wtiles(W):
    K, N = W.shape
    return np.ascontiguousarray(W.reshape(K // 128, 128, N // 128, 128).transpose(2, 1, 0, 3))


def _pvec(v):
    return np.ascontiguousarray(v.reshape(8, 128).T)


def _consts():
    bf = ml_dtypes.bfloat16
    k = np.arange(128)[:, None]
    q = np.arange(128)[None, :]
    prev = np.where(k >= q, 0.0, NEG)
    cur = np.where(k <= q, 0.0, NEG)
    MB = np.concatenate([prev, cur], 1)
    MBN = np.concatenate([np.full((128, 128), 2.0), cur], 1)
    PERM = np.zeros((128, 32))
    for m in range(32):
        PERM[(m + 16) % 32, m] = 1.0
    cbf = np.concatenate([np.eye(128), np.ones((128, 128)), MB, MBN, PERM], 1).astype(bf)
    tri = (k <= q).astype(np.float32)
    invc = np.zeros((128, 4, 16), np.float32)
    for gi, w in enumerate((2, 4, 8, 16)):
        invc[:, gi, :] = 1.0 / np.minimum(np.arange(16) + 1, w)
    inv = (np.float32(500000.0) ** (-np.arange(0, 32, 2, dtype=np.float32) / np.float32(32))).astype(np.float32)
    invf = np.zeros((128, 1), np.float32)
    invf[0:32, 0] = np.concatenate([inv, inv])
    sign = np.zeros((128, 1), np.float32)
    sign[0:16, 0] = -1.0
    sign[16:32, 0] = 1.0
    cf = np.concatenate([tri, invc.reshape(128, 64), invf, sign], 1).astype(np.float32)
    return cbf, cf


_NC_CACHE = {}


def _prep(inputs):
    f = lambda a: np.asarray(a, dtype=np.float32)
    cbf, cf = _consts()
    gains = np.stack([_pvec(f(inputs["g_mem"])), _pvec(f(inputs["even_norm_g"])[0]), _pvec(f(inputs["even_norm_g"])[1]),
                      _pvec(f(inputs["odd_norm_g"])[0]), _pvec(f(inputs["odd_norm_g"])[1])], 1)
    shared = dict(
        gains=np.ascontiguousarray(gains),
        w_in_e=np.stack([_wtiles(f(inputs["even_w_in"])[i]) for i in range(2)]),
        w_in_o=np.stack([_wtiles(f(inputs["odd_w_in"])[i]) for i in range(2)]),
        w_kv_e=np.stack([_wtiles(f(inputs["even_w_mem_kv"])[i]) for i in range(2)]),
        w_kv_o=np.stack([_wtiles(f(inputs["odd_w_mem_kv"])[i]) for i in range(2)]),
        w_out_e=np.stack([_wtiles(f(inputs["even_w_out"])[i]) for i in range(2)]),
        w_out_o=np.stack([_wtiles(f(inputs["odd_w_out"])[i]) for i in range(2)]),
        w_pool=np.ascontiguousarray(f(inputs["even_w_pool"]).transpose(0, 2, 1, 3)),
        pscale=np.ascontiguousarray(f(inputs["even_pool_scale"]).reshape(2, 4, 128).transpose(2, 0, 1)),
        lnp=np.ascontiguousarray(np.stack([f(inputs["odd_ln_g"]).reshape(2, 8, 128), f(inputs["odd_ln_b"]).reshape(2, 8, 128)], 1)
                                 .transpose(3, 0, 1, 2)),
        ws=np.ascontiguousarray(f(inputs["odd_w_s"]).transpose(0, 3, 1, 2)),
        bs=np.ascontiguousarray(f(inputs["odd_b_s"]).reshape(2, 1, 1024)),
        fg=np.ascontiguousarray(f(inputs["final_norm_g"]).reshape(1, D)),
        cbf=cbf, cf=cf,
    )
    x = f(inputs["x"])
    mem = f(inputs["mem"])
    pos = np.asarray(inputs["positions"]).astype(np.int32)
    in_maps = []
    for c in range(8):
        m = dict(shared)
        m["x"] = np.ascontiguousarray(x[c])
        m["mem"] = np.ascontiguousarray(mem[c])
        m["pos"] = np.ascontiguousarray(pos[c].reshape(1, S))
        in_maps.append(m)
    return in_maps


def kernel(**inputs):
    in_maps = _prep(inputs)
    key = "full"
    if key not in _NC_CACHE:
        _NC_CACHE[key] = build()
    nc = _NC_CACHE[key]
    res = run_bass_kernel_spmd(nc, in_maps, core_ids=list(range(8)))
    return np.stack([np.asarray(r["out"], dtype=np.float32) for r in res.results], 0)
```

```python
import contextlib
import numpy as np
import ml_dtypes
import concourse.bass as bass
import concourse.mybir as mybir
from concourse.bass_utils import run_bass_kernel_spmd

F32 = mybir.dt.float32
BF16 = mybir.dt.bfloat16
I32 = mybir.dt.int32
AF = mybir.ActivationFunctionType
ALU = mybir.AluOpType

S = 4096
D = 1024
NST = 8
NTB = 32
EPS = 1e-6
DILS = (1, 4, 16)
N_DSEM = 24
N_DSEM_SW = 8
SCALE = 128 ** -0.5
NEG = -30000.0


class Op:
    __slots__ = ("eng", "fn", "deps", "is_dma", "idx", "signal", "sigcount", "dsem", "dval")


class Prog:
    ENGS = ("pe", "act", "dve", "pool", "sp")

    def __init__(self):
        self.ops = {e: [] for e in self.ENGS}
        self.last_w = {}
        self.readers = {}
        self.dsem_last = [None] * N_DSEM
        self.dsem_cnt = [0] * N_DSEM
        self.n_dma = 0
        self.n_dma_sw = 0
        self.bank_ctr = 0

    def bank(self):
        b = self.bank_ctr % 8
        self.bank_ctr += 1
        return b

    def op(self, eng, fn, reads=(), writes=(), dma=False):
        o = Op()
        o.eng = eng; o.fn = fn; o.is_dma = dma
        o.idx = len(self.ops[eng]); o.signal = False; o.sigcount = 0
        o.dsem = None; o.dval = 0
        deps = []
        for k in reads:
            excl = isinstance(k, str) and k.startswith("ps")
            w = self.last_w.get(k)
            if w is not None:
                deps.append(w)
            if excl:
                deps.extend(r for r in self.readers.get(k, ()) if r.eng != eng)
        for k in writes:
            w = self.last_w.get(k)
            if w is not None:
                deps.append(w)
            deps.extend(self.readers.get(k, ()))
        if dma:
            if eng == "pool":
                s = N_DSEM - N_DSEM_SW + (self.n_dma_sw % N_DSEM_SW)
                self.n_dma_sw += 1
            else:
                s = self.n_dma % (N_DSEM - N_DSEM_SW)
                self.n_dma += 1
            prev = self.dsem_last[s]
            if prev is not None:
                deps.append(prev)
            self.dsem_cnt[s] += 16
            o.dsem = s; o.dval = self.dsem_cnt[s]
            self.dsem_last[s] = o
        fd = []
        seen = set()
        for d in deps:
            if id(d) in seen:
                continue
            seen.add(id(d))
            if (not d.is_dma) and d.eng == eng:
                if eng == "pe" or eng == "sp":
                    continue
                if o.idx - d.idx > 3:
                    continue
            fd.append(d)
            if not d.is_dma:
                d.signal = True
        o.deps = fd
        for k in reads:
            self.readers.setdefault(k, []).append(o)
        for k in writes:
            self.last_w[k] = o
            self.readers[k] = []
        self.ops[eng].append(o)
        return o

    def barrier(self):
        lasts = []
        for e in self.ENGS:
            for o in reversed(self.ops[e]):
                if o.fn is not None and not o.is_dma:
                    lasts.append(o)
                    break
        dl = [d for d in self.dsem_last if d is not None]
        for e in self.ENGS:
            o = Op()
            o.eng = e; o.fn = None; o.is_dma = False
            o.idx = len(self.ops[e]); o.signal = False; o.sigcount = 0
            o.dsem = None; o.dval = 0
            o.deps = [d for d in lasts if d.eng != e] + dl
            for d in o.deps:
                if not d.is_dma:
                    d.signal = True
            self.ops[e].append(o)
        self.last_w = {}
        self.readers = {}

    def emit(self, block, esems, dsems):
        for e in self.ENGS:
            c = 0
            for o in self.ops[e]:
                if o.signal and not o.is_dma:
                    c += 1
                    o.sigcount = c

        def run(ename, eng):
            waited = {}
            for o in self.ops[ename]:
                need = {}
                for d in o.deps:
                    if d.is_dma:
                        key = ("d", d.dsem); val = d.dval
                    else:
                        key = ("e", d.eng); val = d.sigcount
                    if waited.get(key, 0) >= val:
                        continue
                    if need.get(key, 0) < val:
                        need[key] = val
                for key, val in need.items():
                    sem = dsems[key[1]] if key[0] == "d" else esems[key[1]]
                    eng.wait_ge(sem, val)
                    waited[key] = val
                if o.fn is None:
                    continue
                ins = o.fn(eng)
                if o.is_dma:
                    ins.then_inc(dsems[o.dsem], 16)
                elif o.signal:
                    ins.then_inc(esems[ename], 1)

        @block.tensor
        def _(eng):
            run("pe", eng)

        @block.scalar
        def _(eng):
            run("act", eng)

        @block.vector
        def _(eng):
            run("dve", eng)

        @block.gpsimd
        def _(eng):
            run("pool", eng)

        @block.sync
        def _(eng):
            run("sp", eng)


def build(nlayers=4, final_norm=True):
    nc = bass.Bass("TRN2", target_bir_lowering=False)
    dt_in = lambda name, shape, dt: nc.dram_tensor(name, shape, dt, kind="ExternalInput").ap()
    x_in = dt_in("x", [S, D], F32)
    mem_in = dt_in("mem", [256, D], F32)
    pos_in = dt_in("pos", [1, S], I32)
    gains_in = dt_in("gains", [128, 5, 8], F32)
    w_in_e = dt_in("w_in_e", [2, 48, 128, 8, 128], F32)
    w_in_o = dt_in("w_in_o", [2, 32, 128, 8, 128], F32)
    w_kv_e = dt_in("w_kv_e", [2, 8, 128, 8, 128], F32)
    w_kv_o = dt_in("w_kv_o", [2, 8, 128, 8, 128], F32)
    w_out_e = dt_in("w_out_e", [2, 8, 128, 12, 128], F32)
    w_out_o = dt_in("w_out_o", [2, 8, 128, 12, 128], F32)
    w_pool_in = dt_in("w_pool", [2, 128, 4, 128], F32)
    pscale_in = dt_in("pscale", [128, 2, 4], F32)
    lnp_in = dt_in("lnp", [128, 2, 2, 8], F32)
    ws_in = dt_in("ws", [2, 128, 8, 128], F32)
    bs_in = dt_in("bs", [2, 1, 1024], F32)
    fg_in = dt_in("fg", [1, D], F32)
    cbf_in = dt_in("cbf", [128, 128 + 128 + 256 + 256 + 32], BF16)
    cf_in = dt_in("cf", [128, 128 + 64 + 2], F32)
    out = nc.dram_tensor("out", [S, D], F32, kind="ExternalOutput").ap()

    xb = [nc.dram_tensor("xb%d" % i, [S, D], F32).ap() for i in range(2)]
    Vd = nc.dram_tensor("Vd", [S, 512], BF16).ap()
    Yd = nc.dram_tensor("Yd", [12, 128, S], BF16).ap()
    CSd = nc.dram_tensor("CSd", [2, 32, S], F32).ap()

    P = Prog()
    with contextlib.ExitStack() as st:
        esems = {e: st.enter_context(nc.semaphore("es_" + e)) for e in Prog.ENGS}
        dsems = [st.enter_context(nc.semaphore("ds%d" % i)) for i in range(N_DSEM)]
        sb = lambda name, shape, dt: st.enter_context(nc.sbuf_tensor("s_" + name, shape, dt))
        psb = [st.enter_context(nc.psum_tensor("psb%d" % i, [128, 512], F32)) for i in range(8)]

        cbf = sb("cbf", [128, 800], BF16)
        IDENT = cbf[:, 0:128]
        ONES = cbf[:, 128:256]
        MB = cbf[:, 256:512]
        TWOS = cbf[:, 512:640]
        PERM = cbf[:, 768:800]
        cf = sb("cf", [128, 194], F32)
        TRI = cf[:, 0:128]
        INVC = cf[:, 128:192].rearrange("p (g t) -> p g t", g=4)
        INVF = cf[0:32, 192:193]
        SIGN = cf[0:32, 193:194]
        gains = sb("gains", [128, 5, 8], F32)
        memT = sb("memT", [128, 8, 256], BF16)
        hTflat = sb("hT", [128, 8 * S], BF16)
        hT = hTflat[:].rearrange("p (k t) -> p k t", k=8)
        ssn = sb("ssn", [128, 32], F32)
        MKT = sb("MKT", [128, 4, 256], BF16)
        MV = sb("MV", [128, 2, 512], BF16)
        small = sb("small", [128, 64], F32)
        ARN = 65 * 1024
        arena = sb("arena", [128, ARN], BF16)

        class Carve:
            def __init__(self):
                self.off = 0

            def bf(self, n):
                a = arena[:, self.off:self.off + n]
                self.off += n
                assert self.off <= ARN, self.off
                return a

            def f32(self, n):
                return self.bf(2 * n).bitcast(F32)

        block = st.enter_context(nc.Block())

        def dma(out_ap, in_ap, reads=(), writes=(), eng="sp"):
            return P.op(eng, lambda e: e.dma_start(out=out_ap, in_=in_ap), reads, writes, dma=True)

        def mm(out_ap, lhsT, rhs, start, stop, reads, writes, skip=False):
            return P.op("pe", lambda e: e.matmul(out_ap, lhsT=lhsT, rhs=rhs, start=start, stop=stop,
                                                 skip_group_check=skip), reads, writes)

        def act(out_ap, in_ap, func, reads, writes, **kw):
            return P.op("act", lambda e: e.activation(out=out_ap, in_=in_ap, func=func, **kw), reads, writes)

        def tt(eng, out_ap, a, b, op, reads, writes):
            return P.op(eng, lambda e: e.tensor_tensor(out=out_ap, in0=a, in1=b, op=op), reads, writes)

        def ts(eng, out_ap, a, s1, s2, op0, op1, reads, writes):
            if s2 is None:
                return P.op(eng, lambda e: e.tensor_scalar(out=out_ap, in0=a, scalar1=s1, scalar2=None, op0=op0),
                            reads, writes)
            return P.op(eng, lambda e: e.tensor_scalar(out=out_ap, in0=a, scalar1=s1, scalar2=s2, op0=op0, op1=op1),
                        reads, writes)

        def stt(eng, out_ap, a, s, b, op0, op1, reads, writes):
            return P.op(eng, lambda e: e.scalar_tensor_tensor(out=out_ap, in0=a, scalar=s, in1=b, op0=op0, op1=op1),
                        reads, writes)

        def cp(eng, out_ap, in_ap, reads, writes):
            return P.op(eng, lambda e: e.tensor_copy(out=out_ap, in_=in_ap), reads, writes)

        def load_w(src_ap, dst_ap, dkey, gain=None, kc=8):
            dma(dst_ap, src_ap, writes=[dkey], eng="pool")

        class WStream:
            NBUF = 6
            AHEAD = 3

            def __init__(self, items, bufs, gain):
                self.items = items
                self.bufs = bufs
                self.gain = gain
                self.issued = 0
                self.k = 0

            def _issue_to(self, n):
                while self.issued < min(n, len(self.items)):
                    j = self.issued
                    load_w(self.items[j], self.bufs[j % self.NBUF], ("wsb", j % self.NBUF), gain=self.gain)
                    self.issued += 1

            def prefetch(self):
                self._issue_to(self.AHEAD)

            def get(self):
                k = self.k
                self.k += 1
                self._issue_to(k + 1 + self.AHEAD)
                return self.bufs[k % self.NBUF], ("wsb", k % self.NBUF)

        def proj_fm(wb, wkey, stt_, bank):
            for kc in range(8):
                mm(psb[bank][:, :], wb[:, kc, :], hT[:, kc, stt_ * 512:(stt_ + 1) * 512], kc == 0, kc == 7,
                   [wkey], ["ps%d" % bank])

        def rms_rstd(ss_ap, n, key, extra=()):
            ts("dve", ss_ap, ss_ap, 1.0 / D, EPS, ALU.mult, ALU.add, [key] + list(extra), [key])
            act(ss_ap, ss_ap, AF.Sqrt, [key], [key])
            P.op("dve", lambda e: e.reciprocal(out=ss_ap, in_=ss_ap), [key], [key])

        dma(cbf[:], cbf_in[:, :], writes=["cbf"])
        dma(cf[:], cf_in[:, :], writes=["cf"])
        dma(gains[:], gains_in[:, :, :], writes=["gains"])
        P.barrier()

        cv = Carve()
        cv.off = 27 * 1024
        HS = S // 2
        posi = cv.bf(2 * HS).bitcast(I32)
        ang = cv.f32(HS)
        yv = cv.f32(HS)
        ki = cv.bf(2 * HS).bitcast(I32)
        kf = cv.f32(HS)
        tab = cv.f32(HS)
        a32 = lambda t: t[0:32, :]
        rope_thunks = []
        RT = rope_thunks.append
        for hh in range(2):
            RT(lambda hh=hh: dma(posi[0:32, :], pos_in[0:1, hh * HS:(hh + 1) * HS].broadcast_to([32, HS]), writes=["posi"]))
            RT(lambda: cp("dve", ang[0:32, :], posi[0:32, :], ["posi"], ["ang"]))
            RT(lambda: ts("dve", ang[0:32, :], ang[0:32, :], INVF, None, ALU.mult, None, ["ang"], ["ang"]))
            for ti, shift in enumerate((0.75, 0.5)):
                RT(lambda shift=shift: ts("dve", a32(yv), a32(ang), float(1.0 / (2 * np.pi)), shift, ALU.mult, ALU.add, ["ang"], ["yv"]))
                RT(lambda: cp("dve", a32(ki), a32(yv), ["yv"], ["ki"]))
                RT(lambda: cp("dve", a32(kf), a32(ki), ["ki"], ["kf"]))
                RT(lambda: tt("dve", a32(yv), a32(yv), a32(kf), ALU.subtract, ["yv", "kf"], ["yv"]))
                RT(lambda: ts("dve", a32(kf), a32(yv), 0.0, None, ALU.is_lt, None, ["yv"], ["kf"]))
                RT(lambda: tt("dve", a32(yv), a32(yv), a32(kf), ALU.add, ["yv", "kf"], ["yv"]))
                RT(lambda: ts("dve", a32(yv), a32(yv), float(2 * np.pi), -float(np.pi), ALU.mult, ALU.add, ["yv"], ["yv"]))
                if ti == 0:
                    RT(lambda: act(a32(tab), a32(yv), AF.Sin, ["yv"], ["tab"]))
                else:
                    RT(lambda: act(a32(tab), a32(yv), AF.Sin, ["yv"], ["tab"], scale=SIGN))
                RT(lambda ti=ti, hh=hh: dma(CSd[ti, :, hh * HS:(hh + 1) * HS], a32(tab), reads=["tab"], writes=[("CSd", ti, hh)]))

        mx = [cv.f32(D) for _ in range(1)] * 2
        mh = [cv.bf(D) for _ in range(1)] * 2
        assert cv.off <= ARN - 9216
        for mb_ in range(2):
            dma(mx[mb_], mem_in[mb_ * 128:(mb_ + 1) * 128, :], writes=["mx0"])
            ssm = small[:, mb_:mb_ + 1]
            P.op("dve", lambda e, ssm=ssm: e.memset(ssm, 0.0), [], ["ssm%d" % mb_])
            act(mh[mb_], mx[mb_], AF.Square, ["mx0"], ["mh0", "ssm%d" % mb_], accum_out=ssm)
            rms_rstd(ssm, 1, "ssm%d" % mb_)
            ts("dve", mh[mb_], mx[mb_], ssm, None, ALU.mult, None, ["mx0", "ssm%d" % mb_], ["mh0"])
            b = P.bank()
            pt = psb[b][:].bitcast(BF16).rearrange("p (k t) -> p k t", k=8)
            for kc in range(8):
                P.op("pe", lambda e, kc=kc, pt=pt, mb_=mb_: e.transpose(out=pt[:, kc, :], in_=mh[mb_][:, kc * 128:(kc + 1) * 128],
                                                                    identity=IDENT),
                     ["mh0"], ["ps%d" % b])
            tt("dve", memT[:, :, mb_ * 128:(mb_ + 1) * 128], pt, gains[:, 0, :].unsqueeze(2).broadcast_to([128, 8, 128]), ALU.mult,
               ["ps%d" % b], ["memT"])

        def tail_carve(n):
            c = Carve()
            c.off = ARN - n
            return c

        def pass1(xsrc, xkey, gidx, have_stats):
            cv = tail_carve(9216)
            xt = [cv.f32(D) for _ in range(3)]
            hb = [cv.bf(D) for _ in range(3)]
            if not have_stats:
                P.op("dve", lambda e: e.memset(ssn[:], 0.0), [], ["SSN"] + [("ssn", t_) for t_ in range(NTB)])
                for tb in range(NTB):
                    s_ = tb % 3
                    dma(xt[s_], xsrc[tb * 128:(tb + 1) * 128, :], reads=[(xkey, tb)], writes=["xt%d" % s_])
                    act(hb[s_], xt[s_], AF.Square, ["xt%d" % s_], ["hb%d" % s_, ("ssn", tb)], accum_out=ssn[:, tb:tb + 1])
                rms_rstd(ssn[:], 32, "SSN", extra=[("ssn", t_) for t_ in range(NTB)])
            gb = gains[:, gidx, :].unsqueeze(2).broadcast_to([128, 8, 128])
            for tb in range(NTB):
                s_ = tb % 3
                dma(xt[s_], xsrc[tb * 128:(tb + 1) * 128, :], reads=[(xkey, tb)], writes=["xt%d" % s_])
                act(hb[s_], xt[s_], AF.Copy, ["xt%d" % s_, "SSN"], ["hb%d" % s_], scale=ssn[:, tb:tb + 1])
                b = P.bank()
                pt = psb[b][:].bitcast(BF16).rearrange("p (k t) -> p k t", k=8)
                for kc in range(8):
                    P.op("pe", lambda e, kc=kc, pt=pt, s_=s_: e.transpose(out=pt[:, kc, :], in_=hb[s_][:, kc * 128:(kc + 1) * 128],
                                                                     identity=IDENT),
                         ["hb%d" % s_], ["ps%d" % b])
                tt("dve", hT[:, :, tb * 128:(tb + 1) * 128], pt, gb, ALU.mult, ["ps%d" % b], ["hT"])
            P.barrier()

        def mem_kv(wbt):
            for f in range(8):
                s_ = f
                if f < 4:
                    b = P.bank()
                    for kc in range(8):
                        mm(psb[b][:, 0:256], wbt[s_][:, kc, :], memT[:, kc, :], kc == 0, kc == 7, ["wbm%d" % s_], ["ps%d" % b])
                    cp("dve", MKT[:, f, :], psb[b][:, 0:256], ["ps%d" % b], ["MKT"])
                else:
                    b = P.bank()
                    for mb_ in range(2):
                        for kc in range(8):
                            mm(psb[b][:, mb_ * 128:(mb_ + 1) * 128], memT[:, kc, mb_ * 128:(mb_ + 1) * 128], wbt[s_][:, kc, :],
                               mb_ == 0 and kc == 0, kc == 7, ["wbm%d" % s_], ["ps%d" % b], skip=True)
                    cp("dve", MV[:, :, (f - 4) * 128:(f - 3) * 128], psb[b][:, 0:256].rearrange("p (m c) -> p m c", m=2),
                       ["ps%d" % b], ["MV"])

        def gate_from_z(wz, wzkey, stt_, TH, G, gslot):
            b = P.bank()
            proj_fm(wz, wzkey, stt_, b)
            act(TH[gslot], psb[b][:, :], AF.Tanh, ["ps%d" % b], ["TH%d" % gslot], scale=0.5)
            stt("dve", G[gslot], TH[gslot], 1.0, psb[b][:, :], ALU.add, ALU.mult, ["TH%d" % gslot, "ps%d" % b], ["G%d" % gslot])

        def mem_attn(wsm, yi0, bufs):
            QM, PM, RD, TH, G, YO = bufs
            N = 4 * NST
            wts = {}
            stA, stB = {}, {}

            def A(n):
                h, stt_ = divmod(n, NST)
                if stt_ == 0:
                    wts[h] = (wsm.get(), wsm.get())
                (wq, wqk), (wz, wzk) = wts[h]
                sl = n % 2
                b = P.bank()
                proj_fm(wq, wqk, stt_, b)
                act(QM[sl], psb[b][:, :], AF.Copy, ["ps%d" % b], ["QM%d" % sl])
                gate_from_z(wz, wzk, stt_, TH, G, sl)

            def B(n):
                h, stt_ = divmod(n, NST)
                sl = n % 2
                bs_ = [P.bank(), P.bank()]
                for mb_ in range(2):
                    mm(psb[bs_[mb_]][:, :], MKT[:, h, mb_ * 128:(mb_ + 1) * 128], QM[sl], True, True,
                       ["MKT", "QM%d" % sl], ["ps%d" % bs_[mb_]])
                    act(PM[sl][:, mb_, :], psb[bs_[mb_]][:, :], AF.Exp, ["ps%d" % bs_[mb_]], ["PM%d_%d" % (sl, mb_)], scale=SCALE)

            def C(n):
                h, stt_ = divmod(n, NST)
                sl = n % 2
                bn, bd = P.bank(), P.bank()
                for mb_ in range(2):
                    mm(psb[bn][:, :], MV[:, mb_, h * 128:(h + 1) * 128], PM[sl][:, mb_, :], mb_ == 0, mb_ == 1,
                       ["MV", "PM%d_%d" % (sl, mb_)], ["ps%d" % bn])
                for mb_ in range(2):
                    mm(psb[bd][:, :], TWOS, PM[sl][:, mb_, :], mb_ == 0, mb_ == 1,
                       ["PM%d_%d" % (sl, mb_)], ["ps%d" % bd])
                P.op("dve", lambda e, sl=sl, bd=bd: e.reciprocal(out=RD[sl], in_=psb[bd][:, :]), ["ps%d" % bd], ["RD%d" % sl])
                tt("dve", RD[sl], psb[bn][:, :], RD[sl], ALU.mult, ["ps%d" % bn, "RD%d" % sl], ["RD%d" % sl])
                tt("dve", YO[sl], RD[sl], G[sl], ALU.mult, ["RD%d" % sl, "G%d" % sl], ["YO%d" % sl])
                dma(Yd[yi0 + h, :, stt_ * 512:(stt_ + 1) * 512], YO[sl], reads=["YO%d" % sl], writes=[("Yd", yi0 + h, stt_)])

            A(0)
            B(0)
            A(1)
            for n in range(N):
                C(n)
                if n + 1 < N:
                    B(n + 1)
                if n + 2 < N:
                    A(n + 2)

        def pass3(xsrc, xkey, WO, xdst, dkey, last):
            YT = [hTflat[:, 12288 + i * 6144:12288 + (i + 1) * 6144].rearrange("p (k t) -> p k t", k=12) for i in range(2)]
            XT = [hTflat[:, 24576 + i * 2048:24576 + (i + 1) * 2048].bitcast(F32) for i in range(2)]
            XO = [hTflat[:, 28672 + i * 2048:28672 + (i + 1) * 2048].bitcast(F32) for i in range(2)]
            cv = tail_carve(4096)
            FG = cv.f32(D)
            JUNK = cv.f32(D)
            if last:
                dma(FG, fg_in[0:1, :].broadcast_to([128, D]), writes=["FG"])
            else:
                P.op("dve", lambda e: e.memset(ssn[:], 0.0), [], ["SSN"] + [("ssn", t_) for t_ in range(NTB)])
            def ld_y(stt_):
                ys = stt_ % 2
                dma(YT[ys], Yd[:, :, stt_ * 512:(stt_ + 1) * 512].rearrange("k p t -> p k t"),
                    reads=[("Yd", k, stt_) for k in range(12)], writes=["YT%d" % ys])

            def ld_x(tb):
                dma(XT[tb % 2], xsrc[tb * 128:(tb + 1) * 128, :], reads=[(xkey, tb)], writes=["XT%d" % (tb % 2)])

            ld_y(0)
            ld_x(0)
            for stt_ in range(NST):
                ys = stt_ % 2
                for t4 in range(4):
                    tb = stt_ * 4 + t4
                    s_ = tb % 2
                    if t4 == 1 and stt_ + 1 < NST:
                        ld_y(stt_ + 1)
                    if tb + 1 < NTB:
                        ld_x(tb + 1)
                    for half in range(2):
                        b = P.bank()
                        for k in range(12):
                            mm(psb[b][:, :], YT[ys][:, k, t4 * 128:(t4 + 1) * 128], WO[:, k, half * 512:(half + 1) * 512],
                               k == 0, k == 11, ["YT%d" % ys] + [("WO", n) for n in range(half * 4, half * 4 + 4)], ["ps%d" % b])
                        tt("dve", XO[s_][:, half * 512:(half + 1) * 512], XT[s_][:, half * 512:(half + 1) * 512], psb[b][:, :],
                           ALU.add, ["XT%d" % s_, "ps%d" % b], [("XO", s_, half)])
                    if last and final_norm:
                        ss = small[:, 16 + s_:17 + s_]
                        P.op("dve", lambda e, ss=ss: e.memset(ss, 0.0), [], ["ssf%d" % s_])
                        act(JUNK, XO[s_], AF.Square, [("XO", s_, 0), ("XO", s_, 1)], ["JUNK", "ssf%d" % s_], accum_out=ss)
                        rms_rstd(ss, 1, "ssf%d" % s_)
                        stt("dve", XO[s_], XO[s_], ss, FG, ALU.mult, ALU.mult, [("XO", s_, 0), ("XO", s_, 1), "ssf%d" % s_, "FG"],
                            [("XO", s_, 0), ("XO", s_, 1)])
                    dma(xdst[tb * 128:(tb + 1) * 128, :], XO[s_], reads=[("XO", s_, 0), ("XO", s_, 1)], writes=[(dkey, tb)])
                    if not last:
                        act(JUNK, XO[s_], AF.Square, [("XO", s_, 0), ("XO", s_, 1)], ["JUNK", ("ssn", tb)],
                            accum_out=ssn[:, tb:tb + 1])
            if not last:
                rms_rstd(ssn[:], 32, "SSN", extra=[("ssn", t_) for t_ in range(NTB)])
            P.barrier()

        def even_pre(i):
            gain = None
            win = w_in_e[i]
            cv = Carve()
            wsb = [cv.bf(1024).rearrange("p (k n) -> p k n", k=8) for _ in range(6)]
            items = []

            def qk_items(u):
                hh, gg = divmod(u, 3)
                return [win[(gg * 2 + 1) * 4 + hh], win[(gg * 2) * 4 + hh]]

            items += qk_items(0)
            for u in range(12):
                if u + 1 < 12:
                    items += qk_items(u + 1)
                if u % 3 == 2:
                    items.append(win[36 + u // 3])
            for gi in range(4):
                items += [win[28 + gi], win[40 + gi]]
            for h in range(4):
                items += [win[32 + h], win[44 + h]]
            wse = WStream(items, wsb, gain)
            TH = [cv.f32(512) for _ in range(2)]
            G = [cv.f32(512) for _ in range(2)]
            RD = [cv.f32(512) for _ in range(2)]
            YO = [cv.bf(512) for _ in range(2)]
            mark = cv.off
            wbt = [cv.bf(1024).rearrange("p (k n) -> p k n", k=8) for _ in range(8)]
            WV = cv.bf(4096).rearrange("p (k n) -> p k n", k=8)
            vst = [cv.bf(512) for _ in range(2)]
            for f in range(8):
                load_w(w_kv_e[i][f], wbt[f], "wbm%d" % f)
            for j in range(4):
                load_w(win[24 + j], WV[:, :, j * 128:(j + 1) * 128], ("WV", j))
            wse.prefetch()
            return (cv, wse, TH, G, RD, YO, mark, wbt, WV, vst)

        def even_pass2(i, state):
            cv, wse, TH, G, RD, YO, mark, wbt, WV, vst = state
            win = w_in_e[i]
            mem_kv(wbt)

            for tb in range(NTB):
                s_ = tb % 2
                b = P.bank()
                for kc in range(8):
                    mm(psb[b][:, :], hT[:, kc, tb * 128:(tb + 1) * 128], WV[:, kc, :], kc == 0, kc == 7,
                       [("WV", j) for j in range(4)], ["ps%d" % b])
                if tb % 2 == 0:
                    act(vst[s_], psb[b][:, :], AF.Copy, ["ps%d" % b], ["vst%d" % s_])
                else:
                    cp("dve", vst[s_], psb[b][:, :], ["ps%d" % b], ["vst%d" % s_])
                dma(Vd[tb * 128:(tb + 1) * 128, :], vst[s_], reads=["vst%d" % s_], writes=[("Vd", tb)])
                for _ in range(2):
                    if rope_thunks:
                        rope_thunks.pop(0)()
            while rope_thunks:
                rope_thunks.pop(0)()

            P.barrier()
            cv.off = mark
            VG = [cv.bf(4096).rearrange("p (j c) -> p j c", j=32) for _ in range(2)]
            KT = [cv.bf(S) for _ in range(2)]
            QT = [cv.bf(S) for _ in range(2)]
            ACCN = cv.f32(S)
            ACCD = cv.f32(S)
            CS = [cv.f32(1024).rearrange("p (a t) -> p a t", a=2) for _ in range(2)]
            TMPN = [cv.bf(512) for _ in range(4)]
            T1 = [cv.f32(512) for _ in range(2)]
            T2 = [cv.f32(512) for _ in range(2)]
            PS_ = [cv.bf(512) for _ in range(4)]
            psctr = [0]
            ropectr = [0]
            kqctr = [0]

            def load_vg(u):
                hh, gg = divmod(u, 3)
                dl = DILS[gg]
                src = Vd[:, hh * 128:(hh + 1) * 128].rearrange("(jj n r) c -> n r jj c", n=128, r=dl)
                dst = VG[u % 2].rearrange("n (r jj) c -> n r jj c", r=dl)
                rd = [("Vd", tb) for tb in range(NTB)]
                for r in range(dl):
                    dma(dst[:, r, :, :], src[:, r, :, :], reads=rd, writes=[("VG", u % 2, r)])

            def QK(unit):
                h, g = divmod(unit, 3)
                dil = DILS[g]
                kb = unit % 2
                wk, wkk = wse.get()
                wq, wqk = wse.get()
                m_ = 512 // dil
                KD, QD = KT[kb], QT[kb]

                def QK_proj(stt_):
                    info = []
                    for which, (wt, wkey, DST, dname) in enumerate(((wk, wkk, KD, "KT"), (wq, wqk, QD, "QT"))):
                        b = P.bank()
                        proj_fm(wt, wkey, stt_, b)
                        tn = (stt_ % 2) * 2 + which
                        act(TMPN[tn], psb[b][:, :], AF.Copy, ["ps%d" % b], ["TMPN%d" % tn])
                        dv = DST.rearrange("p (r s) -> p r s", r=dil)[:, :, stt_ * m_:(stt_ + 1) * m_]
                        sv = psb[b][:, :].rearrange("p (m r) -> p r m", r=dil)
                        act(dv, sv, AF.Copy, ["ps%d" % b], [(dname, kb, stt_, "a")])
                        info.append((b, tn, dv, dname))
                    return info

                def QK_rope(stt_, info):
                    cs = ropectr[0] % 2
                    ropectr[0] += 1
                    dma(CS[cs][0:32, :, :], CSd[:, :, stt_ * 512:(stt_ + 1) * 512].rearrange("a p t -> p a t"),
                        reads=[("CSd", 0), ("CSd", 1)], writes=["CS%d" % cs])
                    for which, (b, tn, dv, dname) in enumerate(info):
                        b2 = P.bank()
                        mm(psb[b2][0:32, :], PERM, TMPN[tn], True, True, ["TMPN%d" % tn], ["ps%d" % b2])
                        tt("dve", T1[which][0:32, :], psb[b][0:32, :], CS[cs][0:32, 0, :], ALU.mult,
                           ["ps%d" % b, "CS%d" % cs], ["T1%d" % which])
                        tt("dve", T2[which][0:32, :], psb[b2][0:32, :], CS[cs][0:32, 1, :], ALU.mult,
                           ["ps%d" % b2, "CS%d" % cs], ["T2%d" % which])
                        tt("pool", dv[0:32], T1[which][0:32, :].rearrange("p (m r) -> p r m", r=dil),
                           T2[which][0:32, :].rearrange("p (m r) -> p r m", r=dil), ALU.add,
                           ["T1%d" % which, "T2%d" % which, (dname, kb, stt_, "a")], [(dname, kb, stt_, "b")])

                infos = {}
                for stt_ in range(NST):
                    infos[stt_] = QK_proj(stt_)
                    if stt_ > 0:
                        QK_rope(stt_ - 1, infos[stt_ - 1])
                QK_rope(NST - 1, infos[NST - 1])

            def ATT(unit):
                h, g = divmod(unit, 3)
                dil = DILS[g]
                nbk = 32 // dil
                kb = unit % 2
                VGu = VG[unit % 2]
                kkeys = [("KT", kb, s2, ab) for s2 in range(NST) for ab in "ab"]
                qkeys = [("QT", kb, s2, ab) for s2 in range(NST) for ab in "ab"]
                vkeys = [("VG", unit % 2, r) for r in range(dil)]
                abanks = {}

                def AS(jj):
                    sbanks = [P.bank(), P.bank()]
                    abanks[jj] = sbanks
                    for jl in range(4):
                        j = jj * 4 + jl
                        has_prev = (j % nbk) != 0
                        sbk = sbanks[jl // 2]
                        so = (jl % 2) * 256
                        first = (jl % 2 == 0)
                        qsl = QT[kb][:, j * 128:(j + 1) * 128]
                        if has_prev:
                            mm(psb[sbk][:, so:so + 256], IDENT, MB, first, False, ["cbf"], ["ps%d" % sbk], skip=True)
                            mm(psb[sbk][:, so:so + 128], KT[kb][:, (j - 1) * 128:j * 128], qsl,
                               False, False, kkeys + qkeys, ["ps%d" % sbk], skip=True)
                        else:
                            mm(psb[sbk][:, so + 128:so + 256], IDENT, MB[:, 128:256], first, False, ["cbf"], ["ps%d" % sbk], skip=True)
                        mm(psb[sbk][:, so + 128:so + 256], KT[kb][:, j * 128:(j + 1) * 128], qsl,
                           False, True, kkeys + qkeys, ["ps%d" % sbk], skip=True)

                def AV(jj):
                    sbanks = abanks.pop(jj)
                    bn, bd = P.bank(), P.bank()
                    for half in range(2):
                        pslot = psctr[0] % 4
                        psctr[0] += 1
                        pk = "PS%d" % pslot
                        PSx = PS_[pslot]
                        act(PSx, psb[sbanks[half]][:, :], AF.Exp, ["ps%d" % sbanks[half]], [pk], scale=SCALE)
                        for jl2 in range(2):
                            jl = half * 2 + jl2
                            j = jj * 4 + jl
                            has_prev = (j % nbk) != 0
                            so = jl2 * 256
                            first = (jl == 0)
                            last_ = (jl == 3)
                            if has_prev:
                                mm(psb[bn][:, jl * 128:(jl + 1) * 128], VGu[:, j - 1, :], PSx[:, so:so + 128], first, False,
                                   vkeys + [pk], ["ps%d" % bn], skip=True)
                            mm(psb[bn][:, jl * 128:(jl + 1) * 128], VGu[:, j, :], PSx[:, so + 128:so + 256],
                               first and not has_prev, last_, vkeys + [pk], ["ps%d" % bn], skip=True)
                            if has_prev:
                                mm(psb[bd][:, jl * 128:(jl + 1) * 128], TWOS, PSx[:, so:so + 128], first, False,
                                   [pk], ["ps%d" % bd], skip=True)
                            mm(psb[bd][:, jl * 128:(jl + 1) * 128], TWOS, PSx[:, so + 128:so + 256],
                               first and not has_prev, last_, [pk], ["ps%d" % bd], skip=True)
                    spr = S // dil
                    run = min(512, spr)
                    for q in range(512 // run):
                        sig0 = jj * 512 + q * run
                        r = sig0 // spr
                        s0 = sig0 % spr
                        for (ACC, bnk, key) in ((ACCN, bn, "ACCN"), (ACCD, bd, "ACCD")):
                            dstv = ACC.rearrange("p (s r) -> p r s", r=dil)[:, r, s0:s0 + run]
                            srcv = psb[bnk][:, q * run:(q + 1) * run]
                            if g == 0:
                                cp("dve", dstv, srcv, ["ps%d" % bnk], [key])
                            else:
                                tt("dve", dstv, dstv, srcv, ALU.add, ["ps%d" % bnk, key], [key])

                AS(0)
                for jj in range(8):
                    if jj + 1 < 8:
                        AS(jj + 1)
                    AV(jj)

            def FIN(h):
                wz, wzk = wse.get()
                for stt_ in range(NST):
                    sl = stt_ % 2
                    c0, c1 = stt_ * 512, (stt_ + 1) * 512
                    P.op("dve", lambda e, sl=sl, c0=c0, c1=c1: e.reciprocal(out=RD[sl], in_=ACCD[:, c0:c1]), ["ACCD"], ["RD%d" % sl])
                    tt("pool", RD[sl], ACCN[:, c0:c1], RD[sl], ALU.mult, ["ACCN", "RD%d" % sl], ["RD%d" % sl])
                    gate_from_z(wz, wzk, stt_, TH, G, sl)
                    tt("pool", YO[sl], RD[sl], G[sl], ALU.mult, ["RD%d" % sl, "G%d" % sl], ["YO%d" % sl])
                    dma(Yd[h, :, c0:c1], YO[sl], reads=["YO%d" % sl], writes=[("Yd", h, stt_)])

            load_vg(0)
            QK(0)
            for unit in range(12):
                if unit + 1 < 12:
                    QK(unit + 1)
                    load_vg(unit + 1)
                ATT(unit)
                if unit % 3 == 2:
                    FIN(unit // 3)

            P.barrier()
            cv.off = mark
            QM = [cv.bf(512) for _ in range(2)]
            PM = [cv.bf(1024).rearrange("p (m t) -> p m t", m=2) for _ in range(2)]
            WP = cv.bf(512).rearrange("p (g d) -> p g d", g=4)
            dma(WP, w_pool_in[i], writes=["WP"], eng="pool")
            WO = cv.bf(12288).rearrange("p (k n) -> p k n", k=12)
            for n in range(8):
                load_w(w_out_e[i][n], WO[:, :, n * 128:(n + 1) * 128], ("WO", n), kc=12)
            PSC = cv.f32(4)
            dma(PSC, pscale_in[:, i, :], writes=["PSC"])
            ts("dve", PSC, PSC, 0.5, None, ALU.mult, None, ["PSC"], ["PSC"])
            XB = cv.f32(528)
            LV = [cv.f32(528) for _ in range(4)]
            PL = [cv.bf(512) for _ in range(2)]
            T16 = cv.f32(16)
            pw = {}
            pb2 = {}

            def PA(n):
                gi, stt_ = divmod(n, NST)
                L = gi + 1
                w_ = 2 ** L
                if stt_ == 0:
                    pw[gi] = (wse.get(), wse.get())
                    P.op("dve", lambda e: e.memset(XB[:, 0:16], 0.0), [], ["XB"])
                (wq, wqk), (wz, wzk) = pw[gi]
                sl = n % 2
                b = P.bank()
                proj_fm(wq, wqk, stt_, b)
                if stt_ > 0:
                    cp("dve", XB[:, 0:16], XB[:, 512:528], ["XB"], ["XB"])
                act(XB[:, 16:528], psb[b][:, :], AF.Copy, ["ps%d" % b], ["XB"])
                prev = XB
                for l in range(1, L + 1):
                    lo = 16 - (w_ - 2 ** l)
                    sh = 2 ** (l - 1)
                    tt("dve", LV[l - 1][:, lo:528], prev[:, lo:528], prev[:, lo - sh:528 - sh], ALU.add,
                       ["XB", "LV"], ["LV"])
                    prev = LV[l - 1]
                stt("dve", PL[sl], prev[:, 16:528], 1.0 / w_, XB[:, 16:528], ALU.mult, ALU.subtract, ["LV", "XB"], ["PL%d" % sl])
                if stt_ == 0:
                    tt("dve", T16, prev[:, 16:32], INVC[:, gi, :], ALU.mult, ["LV"], ["T16"])
                    tt("dve", PL[sl][:, 0:16], T16, XB[:, 16:32], ALU.subtract, ["T16", "XB"], ["PL%d" % sl])
                gate_from_z(wz, wzk, stt_, TH, G, sl)

            def PB(n):
                gi, stt_ = divmod(n, NST)
                sl = n % 2
                c0, c1 = stt_ * 512, (stt_ + 1) * 512
                b2 = P.bank()
                mm(psb[b2][:, :], WP[:, gi, :], PL[sl], True, True, ["WP", "PL%d" % sl], ["ps%d" % b2])
                stt("dve", YO[sl], psb[b2][:, :], PSC[:, gi:gi + 1], G[sl], ALU.mult, ALU.mult,
                    ["ps%d" % b2, "PSC", "G%d" % sl], ["YO%d" % sl])
                dma(Yd[4 + gi, :, c0:c1], YO[sl], reads=["YO%d" % sl], writes=[("Yd", 4 + gi, stt_)])

            NP_ = 4 * NST
            PA(0)
            for n in range(NP_):
                if n + 1 < NP_:
                    PA(n + 1)
                PB(n)

            mem_attn(wse, 8, (QM, PM, RD, TH, G, YO))
            P.barrier()
            return WO

        def odd_pre(i):
            gain = None
            win = w_in_o[i]
            cv = Carve()
            wvoff = cv.off
            WV = cv.bf(8192).rearrange("p (k n) -> p k n", k=8)
            WU = [cv.bf(1024).rearrange("p (k n) -> p k n", k=8) for _ in range(8)]
            WZ = [cv.bf(1024).rearrange("p (k n) -> p k n", k=8) for _ in range(8)]
            wbtoff = cv.off
            wbt = [cv.bf(1024).rearrange("p (k n) -> p k n", k=8) for _ in range(8)]
            cv.off = wbtoff
            for f in range(8):
                load_w(w_kv_o[i][f], wbt[f], "wbm%d" % f)
            for g in range(8):
                load_w(win[8 + g], WV[:, :, g * 128:(g + 1) * 128], ("WV", g))
            return (cv, wvoff, WV, WU, WZ, wbt)

        def odd_pass2(i, state):
            cv, wvoff, WV, WU, WZ, wbt = state
            gain = None
            win = w_in_o[i]
            mem_kv(wbt)
            P.barrier()
            for g in range(8):
                load_w(win[g], WU[g], ("WU", g))
                load_w(win[20 + g], WZ[g], ("WZ", g))
            WS = cv.bf(1024).rearrange("p (g t) -> p g t", g=8)
            wsst = cv.f32(1024).rearrange("p (g t) -> p g t", g=8)
            dma(wsst, ws_in[i], writes=["wsst"])
            tt("dve", WS, wsst, TRI.unsqueeze(1).broadcast_to([128, 8, 128]), ALU.mult, ["wsst"], ["WS"])
            LNP = cv.f32(16).rearrange("p (a g) -> p a g", a=2)
            dma(LNP, lnp_in[:, i, :, :], writes=["LNP"])
            BSB = cv.f32(1024)
            dma(BSB, bs_in[i, 0:1, :].broadcast_to([128, 1024]), writes=["BSB"])
            BIAS2 = cv.f32(1024).rearrange("p (g t) -> p g t", g=8)
            for hf in range(2):
                b = P.bank()
                mm(psb[b][:, :], ONES, WS[:, hf * 4:(hf + 1) * 4, :], True, True, ["WS"], ["ps%d" % b])
                for g4 in range(4):
                    g = hf * 4 + g4
                    stt("dve", BIAS2[:, g, :], psb[b][:, g4 * 128:(g4 + 1) * 128], LNP[:, 1, g:g + 1], BSB[:, g * 128:(g + 1) * 128],
                        ALU.mult, ALU.add, ["ps%d" % b, "LNP", "BSB"], ["BIAS2"])
            VH = [cv.bf(4096).rearrange("p (c f) -> p c f", c=4) for _ in range(2)]
            UT = [cv.bf(512) for _ in range(2)]
            M1 = [cv.f32(512) for _ in range(2)]
            TH = [cv.f32(512) for _ in range(2)]
            G = [cv.f32(512) for _ in range(2)]
            RD = [cv.f32(512) for _ in range(2)]
            YO = [cv.bf(512) for _ in range(2)]
            QM = [cv.bf(512) for _ in range(2)]
            PM = [cv.bf(1024).rearrange("p (m t) -> p m t", m=2) for _ in range(2)]
            wsb = [cv.bf(1024).rearrange("p (k n) -> p k n", k=8) for _ in range(6)]
            items = []
            for h in range(4):
                items += [win[16 + h], win[28 + h]]
            wso = WStream(items, wsb, gain)
            wso.prefetch()
            STATS = [cv.f32(48).rearrange("p (c a s) -> p c a s", c=4, a=2) for _ in range(2)]
            MVR = [cv.f32(8).rearrange("p (c a) -> p c a", c=4) for _ in range(2)]
            RS4 = [cv.f32(4) for _ in range(2)]
            wvkeys = [("WV", g) for g in range(8)]

            def VCH(st_, c):
                vs = st_ % 2
                tb = st_ * 4 + c
                bb = (P.bank(), P.bank())
                for hf, b in enumerate(bb):
                    for kc in range(8):
                        mm(psb[b][:, :], hT[:, kc, tb * 128:(tb + 1) * 128], WV[:, kc, hf * 512:(hf + 1) * 512], kc == 0, kc == 7,
                           wvkeys, ["ps%d" % b])
                    P.op("dve", lambda e, vs=vs, c=c, hf=hf, b=b: e.bn_stats(out=STATS[vs][:, c, hf, :], in_=psb[b][:, :]),
                         ["ps%d" % b], [("STATS", vs, c, hf)])
                P.op("dve", lambda e, vs=vs, c=c: e.bn_aggr(out=MVR[vs][:, c, :], in_=STATS[vs][:, c, :, :]),
                     [("STATS", vs, c, 0), ("STATS", vs, c, 1)], [("MVR", vs, c)])
                rs = RS4[vs][:, c:c + 1]
                ts("dve", rs, MVR[vs][:, c, 1:2], EPS, None, ALU.add, None, [("MVR", vs, c)], [("RS4", vs, c)])
                act(rs, rs, AF.Sqrt, [("RS4", vs, c)], [("RS4", vs, c)])
                P.op("dve", lambda e, rs=rs: e.reciprocal(out=rs, in_=rs), [("RS4", vs, c)], [("RS4", vs, c)])
                for hf, b in enumerate(bb):
                    ts("dve", VH[vs][:, c, hf * 512:(hf + 1) * 512], psb[b][:, :], MVR[vs][:, c, 0:1], rs,
                       ALU.subtract, ALU.mult, ["ps%d" % b, ("MVR", vs, c), ("RS4", vs, c)], [("VH", vs, c, hf)])

            for c in range(4):
                VCH(0, c)
            for stt_ in range(NST):
                vs = stt_ % 2
                for g in range(8):
                    sl = g % 2
                    c0, c1 = stt_ * 512, (stt_ + 1) * 512
                    b = P.bank()
                    proj_fm(WU[g], ("WU", g), stt_, b)
                    act(UT[sl], psb[b][:, :], AF.Copy, ["ps%d" % b], ["UT%d" % sl])
                    bm = P.bank()
                    for c in range(4):
                        mm(psb[bm][:, c * 128:(c + 1) * 128], VH[vs][:, c, g * 128:(g + 1) * 128], WS[:, g, :], c == 0, c == 3,
                           [("VH", vs, c, g // 4), "WS"], ["ps%d" % bm], skip=True)
                    stt("dve", M1[sl].rearrange("p (c t) -> p c t", c=4), psb[bm][:, :].rearrange("p (c t) -> p c t", c=4),
                        LNP[:, 0, g:g + 1], BIAS2[:, g, :].unsqueeze(1).broadcast_to([128, 4, 128]), ALU.mult, ALU.add,
                        ["ps%d" % bm, "LNP", "BIAS2"], ["M1%d" % sl])
                    gate_from_z(WZ[g], ("WZ", g), stt_, TH, G, sl)
                    stt("dve", M1[sl], M1[sl], 0.5, UT[sl], ALU.mult, ALU.mult, ["M1%d" % sl, "UT%d" % sl], ["M1%d" % sl])
                    tt("pool", YO[sl], M1[sl], G[sl], ALU.mult, ["M1%d" % sl, "G%d" % sl], ["YO%d" % sl])
                    dma(Yd[g, :, c0:c1], YO[sl], reads=["YO%d" % sl], writes=[("Yd", g, stt_)])
                    if g % 2 == 1 and stt_ + 1 < NST:
                        VCH(stt_ + 1, g // 2)
            P.barrier()
            cw = Carve()
            cw.off = wvoff
            WO = cw.bf(12288).rearrange("p (k n) -> p k n", k=12)
            assert cw.off <= wvoff + 8192 + 8192
            for n in range(8):
                load_w(w_out_o[i][n], WO[:, :, n * 128:(n + 1) * 128], ("WO", n), kc=12)
            mem_attn(wso, 8, (QM, PM, RD, TH, G, YO))
            P.barrier()
            return WO

        for L in range(nlayers):
            i = L // 2
            xsrc, xkey = (x_in, "xin") if L == 0 else (xb[(L - 1) % 2], "xb%d" % ((L - 1) % 2))
            last = (L == nlayers - 1)
            xdst, dkey = (out, "out") if last else (xb[L % 2], "xb%d" % (L % 2))
            if L % 2 == 0:
                state = even_pre(i)
                pass1(xsrc, xkey, 1 + i, L > 0)
                WO = even_pass2(i, state)
            else:
                state = odd_pre(i)
                pass1(xsrc, xkey, 3 + i, L > 0)
                WO = odd_pass2(i, state)
            pass3(xsrc, xkey, WO, xdst, dkey, last)
        P.emit(block, esems, dsems)
    return nc


def _wtiles(W):
    K, N = W.shape
    return np.ascontiguousarray(W.reshape(K // 128, 128, N // 128, 128).transpose(2, 1, 0, 3))


def _pvec(v):
    return np.ascontiguousarray(v.reshape(8, 128).T)


def _consts():
    bf = ml_dtypes.bfloat16
    k = np.arange(128)[:, None]
    q = np.arange(128)[None, :]
    prev = np.where(k >= q, 0.0, NEG)
    cur = np.where(k <= q, 0.0, NEG)
    MB = np.concatenate([prev, cur], 1)
    MBN = np.concatenate([np.full((128, 128), 2.0), cur], 1)
    PERM = np.zeros((128, 32))
    for m in range(32):
        PERM[(m + 16) % 32, m] = 1.0
    cbf = np.concatenate([np.eye(128), np.ones((128, 128)), MB, MBN, PERM], 1).astype(bf)
    tri = (k <= q).astype(np.float32)
    invc = np.zeros((128, 4, 16), np.float32)
    for gi, w in enumerate((2, 4, 8, 16)):
        invc[:, gi, :] = 1.0 / np.minimum(np.arange(16) + 1, w)
    inv = (np.float32(500000.0) ** (-np.arange(0, 32, 2, dtype=np.float32) / np.float32(32))).astype(np.float32)
    invf = np.zeros((128, 1), np.float32)
    invf[0:32, 0] = np.concatenate([inv, inv])
    sign = np.zeros((128, 1), np.float32)
    sign[0:16, 0] = -1.0
    sign[16:32, 0] = 1.0
    cf = np.concatenate([tri, invc.reshape(128, 64), invf, sign], 1).astype(np.float32)
    return cbf, cf


_NC_CACHE = {}


def _prep(inputs):
    f = lambda a: np.asarray(a, dtype=np.float32)
    cbf, cf = _consts()
    gains = np.stack([_pvec(f(inputs["g_mem"])), _pvec(f(inputs["even_norm_g"])[0]), _pvec(f(inputs["even_norm_g"])[1]),
                      _pvec(f(inputs["odd_norm_g"])[0]), _pvec(f(inputs["odd_norm_g"])[1])], 1)
    shared = dict(
        gains=np.ascontiguousarray(gains),
        w_in_e=np.stack([_wtiles(f(inputs["even_w_in"])[i]) for i in range(2)]),
        w_in_o=np.stack([_wtiles(f(inputs["odd_w_in"])[i]) for i in range(2)]),
        w_kv_e=np.stack([_wtiles(f(inputs["even_w_mem_kv"])[i]) for i in range(2)]),
        w_kv_o=np.stack([_wtiles(f(inputs["odd_w_mem_kv"])[i]) for i in range(2)]),
        w_out_e=np.stack([_wtiles(f(inputs["even_w_out"])[i]) for i in range(2)]),
        w_out_o=np.stack([_wtiles(f(inputs["odd_w_out"])[i]) for i in range(2)]),
        w_pool=np.ascontiguousarray(f(inputs["even_w_pool"]).transpose(0, 2, 1, 3)),
        pscale=np.ascontiguousarray(f(inputs["even_pool_scale"]).reshape(2, 4, 128).transpose(2, 0, 1)),
        lnp=np.ascontiguousarray(np.stack([f(inputs["odd_ln_g"]).reshape(2, 8, 128), f(inputs["odd_ln_b"]).reshape(2, 8, 128)], 1)
                                 .transpose(3, 0, 1, 2)),
        ws=np.ascontiguousarray(f(inputs["odd_w_s"]).transpose(0, 3, 1, 2)),
        bs=np.ascontiguousarray(f(inputs["odd_b_s"]).reshape(2, 1, 1024)),
        fg=np.ascontiguousarray(f(inputs["final_norm_g"]).reshape(1, D)),
        cbf=cbf, cf=cf,
    )
    x = f(inputs["x"])
    mem = f(inputs["mem"])
    pos = np.asarray(inputs["positions"]).astype(np.int32)
    in_maps = []
    for c in range(8):
        m = dict(shared)
        m["x"] = np.ascontiguousarray(x[c])
        m["mem"] = np.ascontiguousarray(mem[c])
        m["pos"] = np.ascontiguousarray(pos[c].reshape(1, S))
        in_maps.append(m)
    return in_maps


def kernel(**inputs):
    in_maps = _prep(inputs)
    key = "full"
    if key not in _NC_CACHE:
        _NC_CACHE[key] = build()
    nc = _NC_CACHE[key]
    res = run_bass_kernel_spmd(nc, in_maps, core_ids=list(range(8)))
    return np.stack([np.asarray(r["out"], dtype=np.float32) for r in res.results], 0)
```
